# Optimizing a Trainium2 kernel written in Bass

```python
import math
import jax, jax.numpy as jnp
from jax import lax
import numpy as np

D_MODEL = 2048
BATCH = 8
SEQ = 4096
DEPTH = 4

N_EVEN = (DEPTH + 1) // 2
N_ODD = DEPTH // 2
A_CH = D_MODEL // 2
A_GROUPS = 8
CONV_A_WIDTH = 31
B_CH = D_MODEL // 2
B_HEADS = 8
B_HEAD_DIM = B_CH // B_HEADS
CHUNK = 128
MIX_IN = 2 * A_CH + 2 * B_CH
DA_HEADS = 8
DA_HEAD_DIM = 128
DA_V_DIM = 2 * DA_HEAD_DIM
QK_W = DA_HEADS * 2 * DA_HEAD_DIM
V_W = DA_HEADS * DA_V_DIM
Q_BLOCK = 128
D_FF = 5632
FFN_CONV_WIDTH = 3
EPS = 1e-6
NEG_INF = -1e30

kernel_name = "hybrid_conformer_gmlp_diffattn_convffn"


def rmsnorm(x, g):
    xf = x.astype(jnp.float32)
    y = xf * lax.rsqrt(jnp.mean(xf * xf, axis=-1, keepdims=True) + EPS)
    return (y * g.astype(jnp.float32)).astype(x.dtype)


def layernorm(x, g, b):
    xf = x.astype(jnp.float32)
    mu = jnp.mean(xf, axis=-1, keepdims=True)
    xc = xf - mu
    var = jnp.mean(xc * xc, axis=-1, keepdims=True)
    y = xc * lax.rsqrt(var + EPS) * g.astype(jnp.float32) + b.astype(jnp.float32)
    return y.astype(x.dtype)


def causal_dwconv(x, w, b):
    width = w.shape[0]
    y = lax.conv_general_dilated(
        x, w[:, None, :].astype(x.dtype), window_strides=(1,),
        padding=[(width - 1, 0)], dimension_numbers=("NWC", "WIO", "NWC"),
        feature_group_count=x.shape[-1])
    return y + b.astype(x.dtype)


def conv_gmlp_mixer(h, w_in, conv_w, conv_b, ln_a_g, ln_a_b, ln_v_g, ln_v_b, w_s, b_s, w_out):
    bsz, seq, _ = h.shape
    z = h @ w_in
    a_val, a_gate, u, v = jnp.split(z, [A_CH, 2 * A_CH, 2 * A_CH + B_CH], axis=-1)
    a = a_val * jax.nn.sigmoid(a_gate)
    a = causal_dwconv(a, conv_w, conv_b)
    a = jax.nn.silu(layernorm(a, ln_a_g, ln_a_b))
    u = jax.nn.gelu(u)
    v = layernorm(jax.nn.gelu(v), ln_v_g, ln_v_b)
    v = v.reshape(bsz, seq // CHUNK, CHUNK, B_HEADS, B_HEAD_DIM)
    causal = jnp.tril(jnp.ones((CHUNK, CHUNK), dtype=bool))
    w_m = jnp.where(causal[None], w_s, jnp.zeros_like(w_s)).astype(v.dtype)
    s = jnp.einsum("hts,bcshd->bcthd", w_m, v) + b_s.T.astype(v.dtype)[None, None, :, :, None]
    bo = u * s.reshape(bsz, seq, B_CH)
    return jnp.concatenate([a, bo], axis=-1) @ w_out


def diff_attention(h, w_qkv, lq1, lk1, lq2, lk2, subln_g, w_o, lambda_init):
    bsz, seq, _ = h.shape
    nb = seq // Q_BLOCK
    qkv = h @ w_qkv
    q, k, v = jnp.split(qkv, [QK_W, 2 * QK_W], axis=-1)
    q = q.reshape(bsz, seq, DA_HEADS, 2, DA_HEAD_DIM)
    k = k.reshape(bsz, seq, DA_HEADS, 2, DA_HEAD_DIM)
    v = v.reshape(bsz, seq, DA_HEADS, DA_V_DIM)
    lam = (jnp.exp(jnp.sum(lq1.astype(jnp.float32) * lk1.astype(jnp.float32)))
           - jnp.exp(jnp.sum(lq2.astype(jnp.float32) * lk2.astype(jnp.float32)))
           + lambda_init)
    scale = 1.0 / math.sqrt(DA_HEAD_DIM)
    kpos = jnp.arange(seq)
    qblocks = jnp.moveaxis(q.reshape(bsz, nb, Q_BLOCK, DA_HEADS, 2, DA_HEAD_DIM), 1, 0)

    def attend_block(args):
        i, qb = args
        sc = jnp.einsum("bqhcd,bkhcd->bhcqk", qb, k,
                        preferred_element_type=jnp.float32) * scale
        qpos = i * Q_BLOCK + jnp.arange(Q_BLOCK)
        mask = kpos[None, :] <= qpos[:, None]
        sc = jnp.where(mask, sc, NEG_INF)
        p = jax.nn.softmax(sc, axis=-1)
        a = p[:, :, 0] - lam * p[:, :, 1]
        return jnp.einsum("bhqk,bkhe->bqhe", a.astype(v.dtype), v)

    o = lax.map(attend_block, (jnp.arange(nb), qblocks))
    o = jnp.moveaxis(o, 0, 1).reshape(bsz, seq, DA_HEADS, DA_V_DIM)
    o = rmsnorm(o, subln_g) * (1.0 - lambda_init)
    return o.reshape(bsz, seq, V_W) @ w_o


def conv_ffn(h, w_up, conv_w, conv_b, w_down):
    z = causal_dwconv(h @ w_up, conv_w, conv_b)
    gate, val = jnp.split(z, [D_FF], axis=-1)
    return (jax.nn.silu(gate) * val) @ w_down


def setup_inputs(seed: int = 0) -> dict:
    key = jax.random.key(seed)
    ks = jax.random.split(key, 32)
    f32 = jnp.float32

    def nrm(k, shape, scale):
        return jax.random.normal(k, shape, f32) * scale

    return {
        "x": nrm(ks[0], (BATCH, SEQ, D_MODEL), 1.0),
        "norm_mix_g": 1.0 + nrm(ks[1], (DEPTH, D_MODEL), 0.02),
        "norm_ffn_g": 1.0 + nrm(ks[2], (DEPTH, D_MODEL), 0.02),
        "final_norm_g": 1.0 + nrm(ks[3], (D_MODEL,), 0.02),
        "ev_w_in": nrm(ks[4], (N_EVEN, D_MODEL, MIX_IN), D_MODEL ** -0.5),
        "ev_conv_w": nrm(ks[5], (N_EVEN, CONV_A_WIDTH, A_CH), CONV_A_WIDTH ** -0.5),
        "ev_conv_b": nrm(ks[6], (N_EVEN, A_CH), 0.02),
        "ev_ln_a_g": 1.0 + nrm(ks[7], (N_EVEN, A_CH), 0.02),
        "ev_ln_a_b": nrm(ks[8], (N_EVEN, A_CH), 0.02),
        "ev_ln_v_g": 1.0 + nrm(ks[9], (N_EVEN, B_CH), 0.02),
        "ev_ln_v_b": nrm(ks[10], (N_EVEN, B_CH), 0.02),
        "ev_w_s": nrm(ks[11], (N_EVEN, B_HEADS, CHUNK, CHUNK), CHUNK ** -0.5),
        "ev_b_s": 1.0 + nrm(ks[12], (N_EVEN, B_HEADS, CHUNK), 0.02),
        "ev_w_out": nrm(ks[13], (N_EVEN, A_CH + B_CH, D_MODEL), (A_CH + B_CH) ** -0.5),
        "od_w_qkv": nrm(ks[14], (N_ODD, D_MODEL, 2 * QK_W + V_W), D_MODEL ** -0.5),
        "od_lambda_q1": nrm(ks[15], (N_ODD, DA_HEAD_DIM), 0.1),
        "od_lambda_k1": nrm(ks[16], (N_ODD, DA_HEAD_DIM), 0.1),
        "od_lambda_q2": nrm(ks[17], (N_ODD, DA_HEAD_DIM), 0.1),
        "od_lambda_k2": nrm(ks[18], (N_ODD, DA_HEAD_DIM), 0.1),
        "od_subln_g": 1.0 + nrm(ks[19], (N_ODD, DA_V_DIM), 0.02),
        "od_w_o": nrm(ks[20], (N_ODD, V_W, D_MODEL), V_W ** -0.5),
        "ffn_w_up": nrm(ks[21], (DEPTH, D_MODEL, 2 * D_FF), D_MODEL ** -0.5),
        "ffn_conv_w": nrm(ks[22], (DEPTH, FFN_CONV_WIDTH, 2 * D_FF), FFN_CONV_WIDTH ** -0.5),
        "ffn_conv_b": nrm(ks[23], (DEPTH, 2 * D_FF), 0.02),
        "ffn_w_down": nrm(ks[24], (DEPTH, D_FF, D_MODEL), D_FF ** -0.5),
    }


def reference(x, norm_mix_g, norm_ffn_g, final_norm_g,
              ev_w_in, ev_conv_w, ev_conv_b, ev_ln_a_g, ev_ln_a_b, ev_ln_v_g, ev_ln_v_b,
              ev_w_s, ev_b_s, ev_w_out,
              od_w_qkv, od_lambda_q1, od_lambda_k1, od_lambda_q2, od_lambda_k2, od_subln_g, od_w_o,
              ffn_w_up, ffn_conv_w, ffn_conv_b, ffn_w_down):
    for i in range(DEPTH):
        j = i // 2
        h = rmsnorm(x, norm_mix_g[i])
        if i % 2 == 0:
            x = x + conv_gmlp_mixer(h, ev_w_in[j], ev_conv_w[j], ev_conv_b[j],
                                    ev_ln_a_g[j], ev_ln_a_b[j], ev_ln_v_g[j], ev_ln_v_b[j],
                                    ev_w_s[j], ev_b_s[j], ev_w_out[j])
        else:
            lambda_init = 0.8 - 0.6 * math.exp(-0.3 * i)
            x = x + diff_attention(h, od_w_qkv[j], od_lambda_q1[j], od_lambda_k1[j],
                                   od_lambda_q2[j], od_lambda_k2[j], od_subln_g[j], od_w_o[j],
                                   lambda_init)
        h = rmsnorm(x, norm_ffn_g[i])
        x = x + conv_ffn(h, ffn_w_up[i], ffn_conv_w[i], ffn_conv_b[i], ffn_w_down[i])
    return rmsnorm(x, final_norm_g)
```

```python
import contextlib
import math
import numpy as np
import ml_dtypes
import concourse.bass as bass
import concourse.mybir as mybir
from concourse.bass_utils import run_bass_kernel_spmd

F32 = mybir.dt.float32
BF16 = mybir.dt.bfloat16
AF = mybir.ActivationFunctionType
ALU = mybir.AluOpType
AX = mybir.AxisListType

PE, ACT, DVE, POOL, SP = "pe", "act", "dve", "pool", "sp"
ENGS = (PE, ACT, DVE, POOL, SP)

S = 4096
D = 2048
T = 512
NT = S // T
KC = 16
DFF = 5632
NFF = DFF // 128
EPS = 1e-6
SCALE = 1.0 / math.sqrt(128.0)
NSLOT = 5
NKV = 4

C_MIXG, C_FFNG, C_FING, C_CW, C_CB, C_LAG, C_LAB, C_FW, C_FB, NFM = 0, 64, 128, 144, 640, 656, 672, 688, 1744, 2096
B_EV, B_SUB, B_LAM, NBC = 0, 6144, 6656, 7680


class Prog:
    def __init__(self, nc, stack):
        self.nc = nc
        self.stack = stack
        self.ops = {e: [] for e in ENGS}
        self.esem = {e: stack.enter_context(nc.semaphore("prog_" + e)) for e in (PE, ACT, DVE, POOL)}
        self.cnt = {e: 0 for e in (PE, ACT, DVE, POOL)}
        self.dsem = {}
        self.dcnt = {}
        self.waited = {e: {} for e in ENGS}
        self.last_w = {}
        self.readers = {}
        self.region_dma = []
        self.sim = {e: [] for e in ENGS}

    def dma_sem(self, name):
        if name not in self.dsem:
            self.dsem[name] = self.stack.enter_context(self.nc.semaphore("dma_" + name))
            self.dcnt[name] = 0
        return self.dsem[name]

    def wait(self, eng, tok):
        if tok is None:
            return
        kind, name, val = tok
        if kind == "e" and name == PE and eng == PE:
            return
        key = (kind, name)
        if self.waited[eng].get(key, 0) >= val:
            return
        self.waited[eng][key] = val
        sem = self.esem[name] if kind == "e" else self.dsem[name]
        self.ops[eng].append(lambda e, sem=sem, val=val: e.wait_ge(sem, val))
        self.sim[eng].append(("w", key, val))

    def _deps(self, eng, reads, writes, extra):
        for k in reads:
            self.wait(eng, self.last_w.get(k))
        for k in writes:
            self.wait(eng, self.last_w.get(k))
            for t in self.readers.get(k, ()):
                self.wait(eng, t)
        for t in extra:
            self.wait(eng, t)

    def _commit(self, tok, reads, writes):
        for k in reads:
            self.readers.setdefault(k, []).append(tok)
        for k in writes:
            self.last_w[k] = tok
            self.readers[k] = []

    def op(self, eng, fn, reads=(), writes=(), signal=True, extra=()):
        self._deps(eng, reads, writes, extra)
        if signal:
            self.cnt[eng] += 1
            tok = ("e", eng, self.cnt[eng])
            sem = self.esem[eng]
            self.ops[eng].append(lambda e, fn=fn, sem=sem: fn(e).then_inc(sem, 1))
            self.sim[eng].append(("i", ("e", eng), 1))
        else:
            tok = ("e", eng, self.cnt[eng] + 1)
            self.ops[eng].append(lambda e, fn=fn: fn(e))
        self._commit(tok, reads, writes)
        return tok

    def dma(self, q, semname, fn, reads=(), writes=(), extra=(), region=False):
        self._deps(q, reads, writes, extra)
        sem = self.dma_sem(semname)
        self.dcnt[semname] += 16
        tok = ("d", semname, self.dcnt[semname])
        self.ops[q].append(lambda e, fn=fn, sem=sem: fn(e).then_inc(sem, 16))
        self.sim[q].append(("i", ("d", semname), 16))
        self._commit(tok, reads, writes)
        if region:
            self.region_dma.append(tok)
        return tok

    def last_tok(self, eng):
        return ("e", eng, self.cnt[eng]) if self.cnt[eng] else None

    def barrier(self):
        toks = [self.last_tok(e) for e in (PE, ACT, DVE, POOL)] + list(self.region_dma)
        self.region_dma = []
        for eng in (PE, ACT, DVE, SP):
            for t in toks:
                self.wait(eng, t)

    def check_deadlock(self):
        pos = {e: 0 for e in ENGS}
        val = {}
        progress = True
        while progress:
            progress = False
            for e in ENGS:
                lst = self.sim[e]
                while pos[e] < len(lst):
                    k, key, v = lst[pos[e]]
                    if k == "w":
                        if val.get(key, 0) < v:
                            break
                    else:
                        val[key] = val.get(key, 0) + v
                    pos[e] += 1
                    progress = True
        stuck = {e: (pos[e], len(self.sim[e]), self.sim[e][pos[e]]) for e in ENGS if pos[e] < len(self.sim[e])}
        if stuck:
            raise RuntimeError("DEADLOCK in recorded program: %r" % (stuck,))

    def emit(self, block):
        self.check_deadlock()
        prog = self

        @block.tensor
        def _(e):
            for f in prog.ops[PE]:
                f(e)

        @block.scalar
        def _(e):
            for f in prog.ops[ACT]:
                f(e)

        @block.vector
        def _(e):
            for f in prog.ops[DVE]:
                f(e)

        @block.gpsimd
        def _(e):
            for f in prog.ops[POOL]:
                f(e)

        @block.sync
        def _(e):
            for f in prog.ops[SP]:
                f(e)


def lambda_init(i):
    return 0.8 - 0.6 * math.exp(-0.3 * i)


DEBUG = {"mixer": True, "ffn": True, "dump": False}


def weight_schedule(layers):
    seq = []
    for l in layers:
        j = l // 2
        if not DEBUG["mixer"]:
            pass
        elif l % 2 == 0:
            seq += [("w_in", j * 16 + g, 4096) for g in range(16)]
            seq += [("w_dg", j * 8 + g, 4096) for g in range(8)]
            seq += [("w_out", j * 8 + g, 4096) for g in range(8)]
        else:
            seq += [("w_qkv", j * 24 + g, 4096) for g in range(24)]
            seq += [("w_o", j * 8 + g, 4096) for g in range(8)]
        if DEBUG["ffn"]:
            seq += [("w_up", l * 44 + g, 4096) for g in range(44)]
            seq += [("w_down", l * 32 + g, 2816) for g in range(32)]
    return seq


def build_program(layers, final, ntiles=NT):
    nc = bass.Bass("TRN2", target_bir_lowering=False)
    dt = nc.dram_tensor
    x_d = dt("x", [S, D], F32, kind="ExternalInput").ap()
    y_d = dt("y", [S, D], F32, kind="ExternalOutput").ap()
    wshape = {"w_in": (32, 4096), "w_out": (16, 4096), "w_qkv": (48, 4096), "w_o": (16, 4096),
              "w_up": (176, 4096), "w_down": (128, 2816)}
    w_d = {k: dt(k, [g, 128, e], F32, kind="ExternalInput").ap() for k, (g, e) in wshape.items()}
    wb_d = {k: dt("b_" + k, [g, 128, e], BF16, kind="Internal").ap() for k, (g, e) in wshape.items()}
    wb_d["w_dg"] = dt("b_w_dg", [16, 128, 4096], BF16, kind="Internal").ap()
    cfm_d = dt("cfm", [128, NFM], F32, kind="ExternalInput").ap()
    cbc_d = dt("cbc", [1, NBC], F32, kind="ExternalInput").ap()
    wst_d = dt("wst", [16, 128, 128], F32, kind="ExternalInput").ap()
    idm_d = dt("idm", [2, 128, 128], F32, kind="ExternalInput").ap()
    kc_d = dt("kcache", [2, 16, NT, 128, T], BF16, kind="Internal").ap()
    vc_d = dt("vcache", [2, 8, NT, 128, 4 * 260], BF16, kind="Internal").ap()

    with contextlib.ExitStack() as st:
        P = Prog(nc, st)

        def sb(name, shape, dtype):
            return st.enter_context(nc.sbuf_tensor("sb_" + name, shape, dtype))

        x = sb("x", [128, KC, T], F32)
        h = sb("h", [128, KC, T], BF16)
        wslot = [sb("wslot%d" % i, [128, 4096], BF16) for i in range(NSLOT)]
        cfm = sb("cfm_sb", [128, NFM], F32)
        wsm = sb("wsm", [128, 16, 128], BF16)
        ident = sb("ident", [128, 128], F32)
        maskf = sb("maskf", [128, 128], F32)
        maskb = sb("maskb", [128, 128], BF16)
        identb = sb("identb", [128, 128], BF16)
        onesb = sb("onesb", [128, 128], BF16)
        onesf = sb("onesf", [128, 128], F32)
        epsc = sb("epsc", [128, 1], F32)
        sgb = sb("sgb", [128, 2, 256], F32)
        lamw = sb("lamw", [128, 8], F32)
        neglam = sb("neglam", [128, 2], F32)
        halo_f = sb("halo_f", [128, 4 * 88, 2], F32)
        halo_a = sb("halo_a", [128, 2, 8, 30], BF16)
        rstd = sb("rstd", [128, T], F32)
        sqb = [sb("sqb%d" % i, [128, T], BF16) for i in range(2)]
        small = sb("small", [128, 64], F32)
        R = sb("R", [128, 94 * 1024], mybir.dt.uint8)

        def rview(off_kib, shape, dtype):
            n = int(np.prod(shape[1:]))
            esz = 2 if dtype == BF16 else 4
            off = int(off_kib * 1024)
            v = R[:, off:off + n * esz].bitcast(dtype)
            if len(shape) == 3:
                v = v.rearrange("p (a b) -> p a b", a=shape[1])
            elif len(shape) == 4:
                v = v.rearrange("p (a b c) -> p a b c", a=shape[1], b=shape[2])
            return v

        wsf = rview(0, [128, 16, 128], F32)
        lamt = rview(8, [128, 8, 128], F32)
        G = rview(0, [128, NFF, T], BF16)
        zb = [[rview(44 + 4.25 * (2 * b + s_), [128, 520], F32) for s_ in range(2)] for b in range(2)]
        t1 = [[rview(61 + 2 * (2 * b + s_), [128, T], F32) for s_ in range(2)] for b in range(2)]
        qT = rview(0, [128, 16, T], BF16)
        kT = rview(16, [128, 16, T], BF16)
        vcur = rview(32, [128, 4, 8, 260], BF16)
        kslot = [rview(48.25 + i, [128, T], BF16) for i in range(NKV)]
        vslot = [rview(52.25 + 2.25 * i, [128, 4, 260], BF16) for i in range(NKV)]
        PT = [rview(61.25 + i, [128, T], BF16) for i in range(4)]
        O1n = rview(65.25, [128, 4, 256], F32)
        Od = [rview(69.25 + i, [128, 256], F32) for i in range(4)]
        On = [rview(73.25 + i, [128, 256], F32) for i in range(4)]
        oT = rview(77.25, [128, 16, T], BF16)
        abuf = rview(0, [128, 8, 544], BF16)
        dgst = [rview(16 + 8 * i, [128, 32, 128], BF16) for i in range(2)]
        evbc = rview(8.5, [128, 3072], F32)
        ybuf = rview(20.5, [128, 8, T], F32)
        vtok = rview(36.5, [128, 4, 1024], F32)
        vnb = rview(52.5, [128, 4, 1024], BF16)
        AB = rview(60.5, [128, 16, T], BF16)
        wk = [rview(76.5 + 2 * i, [128, T], F32) for i in range(5)]
        xin = rview(0, [128, 4, D], F32)
        xn = rview(32, [128, KC, T], F32)
        yout = xin

        ps = st.enter_context(nc.psum_tensor("ps", [128, 8, 512], F32))
        blk = st.enter_context(nc.Block())

        def dump(name, ap, keys, shape, dtype):
            if not DEBUG["dump"]:
                return
            d = nc.dram_tensor("dbg_" + name, shape, dtype, kind="ExternalOutput").ap()
            P.dma(SP, "dbg", lambda e: e.dma_start(out=d, in_=ap), reads=keys, region=True)

        ps_rr = [0]

        ps_mod = [8]

        def ps_next():
            i = ps_rr[0] % ps_mod[0]
            ps_rr[0] += 1
            return i

        cp_rr = [0]

        def evac_engine():
            cp_rr[0] += 1
            return ACT if cp_rr[0] % 2 else DVE

        P.dma(SP, "c0", lambda e: e.dma_start(out=cfm[:], in_=cfm_d), writes=["cfm"], region=True)
        P.dma(SP, "c1", lambda e: e.dma_start(out=ident[:], in_=idm_d[0]), writes=["ident"], region=True)
        P.dma(SP, "c2", lambda e: e.dma_start(out=maskf[:], in_=idm_d[1]), writes=["maskf"], region=True)
        P.dma(SP, "c3", lambda e: e.dma_start(out=wsf[:], in_=wst_d.rearrange("g s t -> s g t")), writes=["wsf"], region=True)
        P.dma(SP, "c4", lambda e: e.dma_start(out=sgb[:].rearrange("p a b -> p (a b)"),
                                              in_=cbc_d[:, B_SUB:B_SUB + 512].partition_broadcast(128)), writes=["sgb"], region=True)
        P.dma(SP, "c5", lambda e: e.dma_start(out=lamt[:].rearrange("p a b -> p (a b)"),
                                              in_=cbc_d[:, B_LAM:B_LAM + 1024].partition_broadcast(128)), writes=["lamt"], region=True)
        P.op(DVE, lambda e: e.memset(onesb[:], 1.0 / 2048.0), writes=["onesb"])
        P.op(DVE, lambda e: e.memset(onesf[:], 1.0 / 1024.0), writes=["onesf"])
        P.op(DVE, lambda e: e.memset(epsc[:], EPS), writes=["epsc"])
        P.op(DVE, lambda e: e.memset(halo_f[:], 0.0), writes=["halo_f"])
        P.op(DVE, lambda e: e.memset(halo_a[:], 0.0), writes=["halo_a"])
        P.op(DVE, lambda e: e.tensor_copy(out=maskb[:], in_=maskf[:]), reads=["maskf"], writes=["maskb"])
        P.op(DVE, lambda e: e.tensor_copy(out=identb[:], in_=ident[:]), reads=["ident"], writes=["identb"])
        if DEBUG["mixer"]:
            for g in sorted({l // 2 * 8 + i for l in layers if l % 2 == 0 for i in range(8)}):
                stg = dgst[g % 2]
                P.op(DVE, lambda e, stg=stg: e.memset(stg[:, 31, :], 0.0), writes=[("dgst", g % 2)])
                for k in range(31):
                    wc = C_CW + g * 31 + k
                    P.op(DVE, lambda e, stg=stg, k=k, wc=wc: e.tensor_scalar(out=stg[:, k, :], in0=identb[:], scalar1=cfm[:, wc:wc + 1], scalar2=None, op0=ALU.mult),
                         reads=["identb", "cfm"], writes=[("dgst", g % 2)])
                P.dma(SP, "dgw", lambda e, stg=stg, g=g: e.dma_start(out=wb_d["w_dg"][g], in_=stg[:].rearrange("p a b -> p (a b)")),
                      reads=[("dgst", g % 2)], writes=[("wbg", g)], region=True)
        for g in range(16):
            P.op(DVE, lambda e, g=g: e.tensor_tensor(out=wsm[:, g, :], in0=wsf[:, g, :], in1=maskf[:], op=ALU.mult),
                 reads=["wsf", "maskf"], writes=["wsm"])
        for j in range(2):
            li = lambda_init(2 * j + 1)
            for a, (qi, ki) in enumerate(((0 + j, 2 + j), (4 + j, 6 + j))):
                P.op(DVE, lambda e, qi=qi, ki=ki: e.tensor_tensor(out=lamt[:, qi, :], in0=lamt[:, qi, :], in1=lamt[:, ki, :], op=ALU.mult),
                     reads=["lamt"], writes=["lamt"])
                P.op(DVE, lambda e, qi=qi, c=2 * j + a: e.reduce_sum(out=lamw[:, c:c + 1], in_=lamt[:, qi, :], axis=AX.X),
                     reads=["lamt"], writes=["lamw"])
                P.op(ACT, lambda e, c=2 * j + a: e.activation(out=lamw[:, c:c + 1], in_=lamw[:, c:c + 1], func=AF.Exp),
                     reads=["lamw"], writes=["lamw"])
            P.op(DVE, lambda e, j=j: e.tensor_tensor(out=neglam[:, j:j + 1], in0=lamw[:, 2 * j + 1:2 * j + 2], in1=lamw[:, 2 * j:2 * j + 1], op=ALU.subtract),
                 reads=["lamw"], writes=["neglam"])
            P.op(DVE, lambda e, j=j, li=li: e.tensor_scalar(out=neglam[:, j:j + 1], in0=neglam[:, j:j + 1], scalar1=-li, scalar2=None, op0=ALU.add),
                 reads=["neglam"], writes=["neglam"])
            P.op(DVE, lambda e, j=j, li=li: e.tensor_scalar(out=sgb[:, j, :], in0=sgb[:, j, :], scalar1=1.0 - li, scalar2=None, op0=ALU.mult),
                 reads=["sgb"], writes=["sgb"])

        sched = weight_schedule(layers)
        full = sched * ntiles
        wpos = [0]
        wissued = [0]

        def w_issue():
            i = wissued[0]
            if i >= len(full):
                return
            kind, g, e_ = full[i]
            s_ = i % NSLOT
            if i < len(sched) and kind != "w_dg":
                P.dma(POOL, "wc%d" % s_, lambda e, kind=kind, g=g, e_=e_, s_=s_: e.dma_start(out=wslot[s_][:, 0:e_], in_=w_d[kind][g]),
                      writes=[("ws", s_)])
                if ntiles > 1:
                    P.dma(SP, "wk%d" % s_, lambda e, kind=kind, g=g, e_=e_, s_=s_: e.dma_start(out=wb_d[kind][g], in_=wslot[s_][:, 0:e_]),
                          reads=[("ws", s_)], writes=[("wbg", kind, g)])
            else:
                P.dma(SP, "w%d" % s_, lambda e, kind=kind, g=g, e_=e_, s_=s_: e.dma_start(out=wslot[s_][:, 0:e_], in_=wb_d[kind][g]),
                      reads=[("wbg", g) if kind == "w_dg" else ("wbg", kind, g)], writes=[("ws", s_)])
            wissued[0] += 1

        def w_next(kind, g):
            i = wpos[0]
            assert full[i][0] == kind and full[i][1] == g, (full[i], kind, g)
            while wissued[0] < min(i + NSLOT, len(full)):
                w_issue()
            wpos[0] += 1
            s_ = i % NSLOT
            return wslot[s_], ("ws", s_)

        def mm_group(out_ap, out_key, pairs):
            n = len(pairs)
            tok = None
            for i, (l_, r_, rk) in enumerate(pairs):
                tok = P.op(PE, lambda e, l_=l_, r_=r_, i=i: e.matmul(out_ap, l_, r_, start=(i == 0), stop=(i == n - 1)),
                           reads=rk, writes=[out_key], signal=(i == n - 1))
            return tok

        def rmsnorm(gcol, out, out_key):
            b = ps_next()
            for c in range(KC):
                sq = sqb[c % 2]
                P.op(ACT, lambda e, c=c, sq=sq: e.activation(out=sq[:], in_=x[:, c, :], func=AF.Square),
                     reads=[("x", c)], writes=[("sqb", c % 2)])
                P.op(PE, lambda e, c=c, sq=sq, b=b: e.matmul(ps[:, b, :], onesb[:], sq[:], start=(c == 0), stop=(c == KC - 1)),
                     reads=[("sqb", c % 2), "onesb"], writes=[("ps", b)], signal=True)
            P.op(ACT, lambda e, b=b: e.activation(out=rstd[:], in_=ps[:, b, :], func=AF.Sqrt, bias=epsc[:], scale=1.0),
                 reads=[("ps", b), "epsc"], writes=["rstd"])
            P.op(DVE, lambda e: e.reciprocal(out=rstd[:], in_=rstd[:]), reads=["rstd"], writes=["rstd"])
            for c in range(KC):
                P.op(DVE, lambda e, c=c: e.scalar_tensor_tensor(out=out[:, c, :], in0=x[:, c, :], scalar=cfm[:, gcol + c:gcol + c + 1],
                                                                in1=rstd[:], op0=ALU.mult, op1=ALU.mult),
                     reads=[("x", c), "rstd", "cfm"], writes=[(out_key, c)])

        def proj_fm(slot, skey, o, rhs_of, rhs_key_of, nk=KC, width=256):
            b = ps_next()
            pairs = [(slot[:, kc * width + o * 128: kc * width + (o + 1) * 128], rhs_of(kc), [skey, rhs_key_of(kc)]) for kc in range(nk)]
            mm_group(ps[:, b, :], ("ps", b), pairs)
            return b

        def resid_add(m, b):
            P.op(DVE, lambda e, m=m, b=b: e.tensor_tensor(out=x[:, m, :], in0=x[:, m, :], in1=ps[:, b, :], op=ALU.add),
                 reads=[("ps", b), ("x", m)], writes=[("x", m)])

        def ffn(l, t=1):
            P.barrier()
            rmsnorm(C_FFNG + l * 16, h, "h")
            if t == 0 and l == layers[0]:
                dump("hffn", h[:], [("h", c) for c in range(KC)], [128, KC, T], BF16)
            for j in range(NFF):
                slot, skey = w_next("w_up", l * 44 + j)
                bb = j % 2
                banks = []
                for s_ in range(2):
                    b = proj_fm(slot, skey, s_, lambda kc: h[:, kc, :], lambda kc: ("h", kc))
                    banks.append(b)
                for s_ in range(2):
                    b = banks[s_]
                    jj = j + 44 * s_
                    z = zb[bb][s_]
                    tt = t1[bb][s_]
                    zk, tk = ("zb", bb, s_), ("t1", bb, s_)
                    hcol = l * 88 + jj
                    wc = C_FW + hcol * 3
                    bc = C_FB + hcol
                    P.op(DVE, lambda e, z=z, hcol=hcol: e.tensor_copy(out=z[:, 0:2], in_=halo_f[:, hcol, :]),
                         reads=[("halo_f", hcol)], writes=[zk])
                    P.op(ACT, lambda e, z=z, b=b: e.activation(out=z[:, 2:2 + T], in_=ps[:, b, :], func=AF.Copy),
                         reads=[("ps", b)], writes=[zk])
                    P.op(ACT, lambda e, tt=tt, b=b, wc=wc, bc=bc: e.activation(out=tt[:], in_=ps[:, b, :], func=AF.Identity,
                                                                              scale=cfm[:, wc + 2:wc + 3], bias=cfm[:, bc:bc + 1]),
                         reads=[("ps", b), "cfm"], writes=[tk])
                    P.op(DVE, lambda e, z=z, hcol=hcol: e.tensor_copy(out=halo_f[:, hcol, :], in_=z[:, T:T + 2]),
                         reads=[zk], writes=[("halo_f", hcol)])
                    P.op(DVE, lambda e, z=z, tt=tt, wc=wc: e.scalar_tensor_tensor(out=tt[:], in0=z[:, 1:1 + T], scalar=cfm[:, wc + 1:wc + 2],
                                                                               in1=tt[:], op0=ALU.mult, op1=ALU.add),
                         reads=[zk, tk, "cfm"], writes=[tk])
                    P.op(DVE, lambda e, z=z, tt=tt, wc=wc: e.scalar_tensor_tensor(out=tt[:], in0=z[:, 0:T], scalar=cfm[:, wc:wc + 1],
                                                                               in1=tt[:], op0=ALU.mult, op1=ALU.add),
                         reads=[zk, tk, "cfm"], writes=[tk])
                tg, tv = t1[bb][0], t1[bb][1]
                P.op(ACT, lambda e, tg=tg: e.activation(out=tg[:], in_=tg[:], func=AF.Silu),
                     reads=[("t1", bb, 0)], writes=[("t1", bb, 0)])
                P.op(DVE, lambda e, tg=tg, tv=tv, j=j: e.tensor_tensor(out=G[:, j, :], in0=tg[:], in1=tv[:], op=ALU.mult),
                     reads=[("t1", bb, 0), ("t1", bb, 1)], writes=[("G", j)])
            if t == 0 and l == layers[0]:
                dump("G", G, [("G", c) for c in range(NFF)], [128, NFF, T], BF16)
            for m in range(KC):
                b = ps_next()
                for half in range(2):
                    slot, skey = w_next("w_down", l * 32 + m * 2 + half)
                    for kk in range(22):
                        kc = half * 22 + kk
                        P.op(PE, lambda e, b=b, slot=slot, kk=kk, kc=kc: e.matmul(ps[:, b, :], slot[:, kk * 128:(kk + 1) * 128], G[:, kc, :],
                                                                                 start=(kc == 0), stop=(kc == 43)),
                             reads=[skey, ("G", kc)], writes=[("ps", b)], signal=(kc == 43))
                resid_add(m, b)

        def even_mixer(l):
            j = l // 2
            P.barrier()
            tok = P.dma(SP, "evbc", lambda e: e.dma_start(out=evbc[:], in_=cbc_d[:, B_EV + j * 3072:B_EV + (j + 1) * 3072].partition_broadcast(128)),
                        writes=["evbc"], region=True)
            rmsnorm(C_MIXG + l * 16, h, "h")
            P.op(DVE, lambda e: e.tensor_copy(out=abuf[:, :, 0:30], in_=halo_a[:, j, :, :]), reads=["halo_a"], writes=[("abuf", i) for i in range(8)])
            for i in range(8):
                slot, skey = w_next("w_in", j * 16 + i)
                bv = proj_fm(slot, skey, 0, lambda kc: h[:, kc, :], lambda kc: ("h", kc))
                bg = proj_fm(slot, skey, 1, lambda kc: h[:, kc, :], lambda kc: ("h", kc))
                w_ = wk[i % 2]
                P.op(ACT, lambda e, w_=w_, bg=bg: e.activation(out=w_[:], in_=ps[:, bg, :], func=AF.Sigmoid),
                     reads=[("ps", bg)], writes=[("wk", i % 2)])
                P.op(DVE, lambda e, w_=w_, bv=bv, i=i: e.tensor_tensor(out=abuf[:, i, 30:30 + T], in0=ps[:, bv, :], in1=w_[:], op=ALU.mult),
                     reads=[("ps", bv), ("wk", i % 2)], writes=[("abuf", i)])
            P.op(DVE, lambda e: e.tensor_copy(out=halo_a[:, j, :, :], in_=abuf[:, :, T:T + 30]), reads=[("abuf", i) for i in range(8)], writes=["halo_a"])
            for vg in range(4):
                slot, skey = w_next("w_in", j * 16 + 8 + vg)
                for tb in range(4):
                    b = ps_next()
                    pairs = [(h[:, kc, tb * 128:(tb + 1) * 128], slot[:, kc * 256:(kc + 1) * 256], [skey, ("h", kc)]) for kc in range(KC)]
                    mm_group(ps[:, b, 0:256], ("ps", b), pairs)
                    P.op(ACT, lambda e, b=b, tb=tb, vg=vg: e.activation(out=vtok[:, tb, vg * 256:(vg + 1) * 256], in_=ps[:, b, 0:256], func=AF.Gelu_apprx_tanh),
                         reads=[("ps", b)], writes=[("vtok", tb, vg)])
            for tb in range(4):
                vk = [("vtok", tb, vg) for vg in range(4)]
                so = tb * 16
                for hh in range(2):
                    P.op(DVE, lambda e, tb=tb, hh=hh, so=so: e.bn_stats(out=small[:, so + hh * 6:so + hh * 6 + 6], in_=vtok[:, tb, hh * 512:(hh + 1) * 512]),
                         reads=vk, writes=[("small", tb)])
                P.op(DVE, lambda e, so=so: e.bn_aggr(out=small[:, so + 12:so + 14], in_=small[:, so:so + 12]), reads=[("small", tb)], writes=[("small", tb)])
                P.op(ACT, lambda e, so=so: e.activation(out=small[:, so + 14:so + 15], in_=small[:, so + 13:so + 14], func=AF.Sqrt, bias=epsc[:], scale=1.0),
                     reads=[("small", tb), "epsc"], writes=[("small", tb)])
                P.op(DVE, lambda e, so=so: e.reciprocal(out=small[:, so + 14:so + 15], in_=small[:, so + 14:so + 15]), reads=[("small", tb)], writes=[("small", tb)])
                P.op(DVE, lambda e, tb=tb, so=so: e.tensor_scalar(out=vtok[:, tb, :], in0=vtok[:, tb, :], scalar1=small[:, so + 12:so + 13],
                                                                 scalar2=small[:, so + 14:so + 15], op0=ALU.subtract, op1=ALU.mult),
                     reads=vk + [("small", tb)], writes=vk)
                P.op(DVE, lambda e, tb=tb: e.tensor_tensor(out=vtok[:, tb, :], in0=vtok[:, tb, :], in1=evbc[:, 0:1024], op=ALU.mult),
                     reads=vk + ["evbc"], writes=vk)
                P.op(DVE, lambda e, tb=tb: e.tensor_tensor(out=vnb[:, tb, :], in0=vtok[:, tb, :], in1=evbc[:, 1024:2048], op=ALU.add),
                     reads=vk + ["evbc"], writes=[("vnb", tb)])
            for ug in range(4):
                slot, skey = w_next("w_in", j * 16 + 12 + ug)
                for o in range(2):
                    b = proj_fm(slot, skey, o, lambda kc: h[:, kc, :], lambda kc: ("h", kc))
                    ci = 8 + 2 * ug + o
                    P.op(ACT, lambda e, b=b, ci=ci: e.activation(out=AB[:, ci, :], in_=ps[:, b, :], func=AF.Gelu_apprx_tanh),
                         reads=[("ps", b)], writes=[("AB", ci)])
            for hd in range(8):
                b = ps_next()
                for tb in range(4):
                    P.op(PE, lambda e, b=b, tb=tb, hd=hd: e.matmul(ps[:, b, tb * 128:(tb + 1) * 128], vnb[:, tb, hd * 128:(hd + 1) * 128],
                                                                   wsm[:, j * 8 + hd, :], start=True, stop=True),
                         reads=[("vnb", tb), "wsm"], writes=[("ps", b)], signal=(tb == 3))
                w_ = wk[2 + hd % 2]
                wkey = ("wk", 2 + hd % 2)
                for tb in range(4):
                    P.op(DVE, lambda e, b=b, tb=tb, hd=hd, w_=w_: e.tensor_tensor(out=w_[:, tb * 128:(tb + 1) * 128], in0=ps[:, b, tb * 128:(tb + 1) * 128],
                                                                              in1=evbc[:, 2048 + hd * 128:2048 + (hd + 1) * 128], op=ALU.add),
                         reads=[("ps", b), "evbc"], writes=[wkey])
                P.op(DVE, lambda e, hd=hd, w_=w_: e.tensor_tensor(out=AB[:, 8 + hd, :], in0=w_[:], in1=AB[:, 8 + hd, :], op=ALU.mult),
                     reads=[wkey, ("AB", 8 + hd)], writes=[("AB", 8 + hd)])
            for i in range(8):
                slot, skey = w_next("w_dg", j * 8 + i)
                b = ps_next()
                pairs = [(slot[:, k * 128:(k + 1) * 128], abuf[:, i, k:k + T], [skey, ("abuf", i)]) for k in range(31)]
                mm_group(ps[:, b, :], ("ps", b), pairs)
                cb = C_CB + j * 8 + i
                P.op(ACT, lambda e, b=b, i=i, cb=cb: e.activation(out=ybuf[:, i, :], in_=ps[:, b, :], func=AF.Identity, bias=cfm[:, cb:cb + 1], scale=1.0),
                     reads=[("ps", b), "cfm"], writes=[("ybuf", i)])
            bm, bq = ps_next(), ps_next()
            for i in range(8):
                w_ = wk[i % 2]
                P.op(ACT, lambda e, i=i, w_=w_: e.activation(out=w_[:], in_=ybuf[:, i, :], func=AF.Square), reads=[("ybuf", i)], writes=[("wk", i % 2)])
                P.op(PE, lambda e, i=i, bm=bm: e.matmul(ps[:, bm, :], onesf[:], ybuf[:, i, :], start=(i == 0), stop=(i == 7)),
                     reads=[("ybuf", i), "onesf"], writes=[("ps", bm)], signal=True)
                P.op(PE, lambda e, i=i, bq=bq, w_=w_: e.matmul(ps[:, bq, :], onesf[:], w_[:], start=(i == 0), stop=(i == 7)),
                     reads=[("wk", i % 2), "onesf"], writes=[("ps", bq)], signal=True)
            mean, var = wk[2], wk[3]
            P.op(ACT, lambda e: e.activation(out=mean[:], in_=ps[:, bm, :], func=AF.Copy), reads=[("ps", bm)], writes=[("wk", 2)])
            P.op(ACT, lambda e: e.activation(out=var[:], in_=ps[:, bm, :], func=AF.Square), reads=[("ps", bm)], writes=[("wk", 3)])
            P.op(DVE, lambda e: e.tensor_tensor(out=var[:], in0=ps[:, bq, :], in1=var[:], op=ALU.subtract), reads=[("ps", bq), ("wk", 3)], writes=[("wk", 3)])
            P.op(ACT, lambda e: e.activation(out=var[:], in_=var[:], func=AF.Sqrt, bias=epsc[:], scale=1.0), reads=[("wk", 3), "epsc"], writes=[("wk", 3)])
            P.op(DVE, lambda e: e.reciprocal(out=var[:], in_=var[:]), reads=[("wk", 3)], writes=[("wk", 3)])
            for i in range(8):
                w_ = wk[i % 2]
                ga, ba = C_LAG + j * 8 + i, C_LAB + j * 8 + i
                P.op(DVE, lambda e, i=i, w_=w_: e.tensor_tensor(out=w_[:], in0=ybuf[:, i, :], in1=mean[:], op=ALU.subtract),
                     reads=[("ybuf", i), ("wk", 2)], writes=[("wk", i % 2)])
                P.op(DVE, lambda e, w_=w_: e.tensor_tensor(out=w_[:], in0=w_[:], in1=var[:], op=ALU.mult),
                     reads=[("wk", i % 2), ("wk", 3)], writes=[("wk", i % 2)])
                P.op(ACT, lambda e, i=i, w_=w_, ga=ga, ba=ba: e.activation(out=AB[:, i, :], in_=w_[:], func=AF.Silu, scale=cfm[:, ga:ga + 1], bias=cfm[:, ba:ba + 1]),
                     reads=[("wk", i % 2), "cfm"], writes=[("AB", i)])
            if l == layers[0] and wpos[0] < len(sched):
                dump("AB", AB, [("AB", c) for c in range(16)], [128, 16, T], BF16)
                dump("hmix", h[:], [("h", c) for c in range(KC)], [128, KC, T], BF16)
                dump("ybuf", ybuf, [("ybuf", c) for c in range(8)], [128, 8, T], F32)
                dump("vnb", vnb, [("vnb", c) for c in range(4)], [128, 4, 1024], BF16)
            for g in range(8):
                slot, skey = w_next("w_out", j * 8 + g)
                for o in range(2):
                    b = proj_fm(slot, skey, o, lambda kc: AB[:, kc, :], lambda kc: ("AB", kc))
                    resid_add(2 * g + o, b)

        def odd_mixer(l, t):
            j = l // 2
            P.barrier()
            rmsnorm(C_MIXG + l * 16, h, "h")
            units = [(hd, c, kt) for hd in range(8) for c in range(2) for kt in range(t + 1)]
            loads = [u for u in units if u[2] < t]
            lpos = [0]
            lslot = {}

            def issue_load():
                i = lpos[0]
                if i >= len(loads):
                    return
                hd, c, kt = loads[i]
                s_ = i % NKV
                tk1 = P.dma(SP, "kl%d" % s_, lambda e, s_=s_, hm=2 * hd + c, kt=kt: e.dma_start(out=kslot[s_][:], in_=kc_d[j, hm, kt]),
                            reads=[("kc", j, kt)], writes=[("kslot", s_)])
                tk2 = P.dma(SP, "vl%d" % s_, lambda e, s_=s_, hd=hd, kt=kt: e.dma_start(out=vslot[s_][:], in_=vc_d[j, hd, kt].rearrange("p (a b) -> p a b", a=4)),
                            reads=[("vc", j, hd, kt)], writes=[("vslot", s_)])
                P.region_dma += [tk1, tk2]
                lslot[loads[i]] = s_
                lpos[0] += 1

            def ensure(i):
                while lpos[0] <= i and lpos[0] < len(loads):
                    issue_load()

            ensure(2)
            for hd in range(8):
                slot, skey = w_next("w_qkv", j * 24 + hd)
                for c in range(2):
                    b = proj_fm(slot, skey, c, lambda kc: h[:, kc, :], lambda kc: ("h", kc))
                    eng = evac_engine()
                    if eng == ACT:
                        P.op(ACT, lambda e, b=b, hm=2 * hd + c: e.activation(out=qT[:, hm, :], in_=ps[:, b, :], func=AF.Copy), reads=[("ps", b)], writes=[("qT", 2 * hd + c)])
                    else:
                        P.op(DVE, lambda e, b=b, hm=2 * hd + c: e.tensor_copy(out=qT[:, hm, :], in_=ps[:, b, :]), reads=[("ps", b)], writes=[("qT", 2 * hd + c)])
            for hd in range(8):
                slot, skey = w_next("w_qkv", j * 24 + 8 + hd)
                for c in range(2):
                    b = proj_fm(slot, skey, c, lambda kc: h[:, kc, :], lambda kc: ("h", kc))
                    eng = evac_engine()
                    if eng == ACT:
                        P.op(ACT, lambda e, b=b, hm=2 * hd + c: e.activation(out=kT[:, hm, :], in_=ps[:, b, :], func=AF.Copy), reads=[("ps", b)], writes=[("kT", 2 * hd + c)])
                    else:
                        P.op(DVE, lambda e, b=b, hm=2 * hd + c: e.tensor_copy(out=kT[:, hm, :], in_=ps[:, b, :]), reads=[("ps", b)], writes=[("kT", 2 * hd + c)])
            if t < ntiles - 1:
                tk_ = P.dma(SP, "kvw", lambda e: e.dma_start(out=kc_d[j, :, t].rearrange("hm d s -> d hm s"), in_=kT[:]),
                            reads=[("kT", hm) for hm in range(16)], writes=[("kc", j, t)])
                P.region_dma.append(tk_)
            if True:
                P.op(DVE, lambda e: e.memset(vcur[:, :, :, 256:257], 1.0), writes=[("vcur", hd) for hd in range(8)])
            for hd in range(8):
                slot, skey = w_next("w_qkv", j * 24 + 16 + hd)
                for tb in range(4):
                    b = ps_next()
                    pairs = [(h[:, kc, tb * 128:(tb + 1) * 128], slot[:, kc * 256:(kc + 1) * 256], [skey, ("h", kc)]) for kc in range(KC)]
                    mm_group(ps[:, b, 0:256], ("ps", b), pairs)
                    eng = ACT if hd % 2 == 0 else DVE
                    if eng == ACT:
                        P.op(ACT, lambda e, b=b, tb=tb, hd=hd: e.activation(out=vcur[:, tb, hd, 0:256], in_=ps[:, b, 0:256], func=AF.Copy), reads=[("ps", b)], writes=[("vcur", hd)])
                    else:
                        P.op(DVE, lambda e, b=b, tb=tb, hd=hd: e.tensor_copy(out=vcur[:, tb, hd, 0:256], in_=ps[:, b, 0:256]), reads=[("ps", b)], writes=[("vcur", hd)])
                if t < ntiles - 1:
                    tk_ = P.dma(SP, "kvw", lambda e, hd=hd: e.dma_start(out=vc_d[j, hd, t].rearrange("p (a b) -> p a b", a=4), in_=vcur[:, :, hd, :]),
                                reads=[("vcur", hd)], writes=[("vc", j, hd, t)])
                    P.region_dma.append(tk_)
            if t == 0 and l == layers[0]:
                dump("qT", qT, [("qT", c) for c in range(16)], [128, 16, T], BF16)
                dump("kT", kT, [("kT", c) for c in range(16)], [128, 16, T], BF16)
                dump("vcur", vcur, [("vcur", c) for c in range(8)], [128, 4, 8, 260], BF16)
            ps_mod[0] = 4
            pt_i = [0]
            def_b, def_t = [], []
            for hd in range(8):
                for c in range(2):
                    hm = 2 * hd + c
                    blocks = [(kt, kb) for kt in range(t + 1) for kb in range(4)]
                    pendq = []
                    nblk = 0

                    def score(kt, kb):
                        diag = (kt == t)
                        q0 = kb * 128 if diag else 0
                        if diag:
                            kap, kkey = kT[:, hm, kb * 128:(kb + 1) * 128], ("kT", hm)
                        else:
                            s_ = lslot[(hd, c, kt)]
                            kap, kkey = kslot[s_][:, kb * 128:(kb + 1) * 128], ("kslot", s_)
                        b = ps_next()
                        P.op(PE, lambda e, b=b, kap=kap, q0=q0, hm=hm: e.matmul(ps[:, b, q0:T], kap, qT[:, hm, q0:T], start=True, stop=True),
                             reads=[kkey, ("qT", hm)], writes=[("ps", b)], signal=True)
                        pi = pt_i[0] % 4
                        pt_i[0] += 1
                        pt = PT[pi]
                        P.op(ACT, lambda e, b=b, pt=pt, q0=q0: e.activation(out=pt[:, q0:T], in_=ps[:, b, q0:T], func=AF.Exp, scale=SCALE),
                             reads=[("ps", b)], writes=[("PT", pi)])
                        if diag:
                            P.op(DVE, lambda e, pt=pt, kb=kb: e.tensor_tensor(out=pt[:, kb * 128:(kb + 1) * 128], in0=pt[:, kb * 128:(kb + 1) * 128], in1=maskb[:], op=ALU.mult),
                                 reads=[("PT", pi), "maskb"], writes=[("PT", pi)])
                        return (kt, kb, pi)

                    def evac_a(qb):
                        so = 20 + qb
                        P.op(DVE, lambda e, qb=qb, so=so: e.reciprocal(out=small[:, so:so + 1], in_=ps[:, 4 + qb, 256:257]), reads=[("ps", 4 + qb)], writes=[("smz", qb)])
                        if c == 0:
                            P.op(DVE, lambda e, qb=qb, so=so: e.tensor_scalar(out=O1n[:, qb, :], in0=ps[:, 4 + qb, 0:256], scalar1=small[:, so:so + 1], scalar2=None, op0=ALU.mult),
                                 reads=[("ps", 4 + qb), ("smz", qb)], writes=[("O1n", qb)])
                        else:
                            P.op(DVE, lambda e, so=so: e.tensor_tensor(out=small[:, so:so + 1], in0=small[:, so:so + 1], in1=neglam[:, j:j + 1], op=ALU.mult),
                                 reads=[("smz", qb), "neglam"], writes=[("smz", qb)])
                            P.op(DVE, lambda e, qb=qb, so=so: e.scalar_tensor_tensor(out=Od[qb][:], in0=ps[:, 4 + qb, 0:256], scalar=small[:, so:so + 1], in1=O1n[:, qb, :],
                                                                                  op0=ALU.mult, op1=ALU.add),
                                 reads=[("ps", 4 + qb), ("smz", qb), ("O1n", qb)], writes=[("Od", qb)])

                    def av(kt, kb, pi):
                        diag = (kt == t)
                        if diag:
                            vap_of, vkey = (lambda kb_: vcur[:, kb_, hd, 0:257]), ("vcur", hd)
                        else:
                            s_ = lslot[(hd, c, kt)]
                            vap_of, vkey = (lambda kb_, s_=s_: vslot[s_][:, kb_, 0:257]), ("vslot", s_)
                        qb0 = kb if diag else 0
                        for qb in range(qb0, 4):
                            first = (kt == 0 and kb == 0)
                            last = (diag and kb == qb)
                            P.op(PE, lambda e, qb=qb, pi=pi, vap=vap_of(kb), first=first, last=last: e.matmul(ps[:, 4 + qb, 0:257], PT[pi][:, qb * 128:(qb + 1) * 128], vap, start=first, stop=last),
                                 reads=[("PT", pi), vkey], writes=[("ps", 4 + qb)], signal=(last or qb == 3))
                        if diag:
                            evac_a(kb)

                    for (kt, kb) in blocks:
                        if kb == 0 and kt < t:
                            ensure((hd * 2 + c) * t + kt + 2)
                        pendq.append(score(kt, kb))
                        if len(pendq) > 2:
                            av(*pendq.pop(0))
                        nblk += 1
                        if nblk == min(3, len(blocks)):
                            dl = def_b if c == 0 else def_t
                            for f_ in dl:
                                f_()
                            dl.clear()
                        if kb == 3 and kt < t:
                            pass
                    while pendq:
                        av(*pendq.pop(0))
                    if c == 1:
                        def phase_b(hd=hd):
                            for qb in range(4):
                                s2 = 28 + qb
                                P.op(DVE, lambda e, s2=s2: e.memset(small[:, s2:s2 + 1], 0.0), writes=[("sms", qb)])
                            for qb in range(4):
                                s2 = 28 + qb
                                P.op(ACT, lambda e, qb=qb, s2=s2: e.activation(out=On[qb][:], in_=Od[qb][:], func=AF.Square, accum_out=small[:, s2:s2 + 1]),
                                     reads=[("Od", qb)], writes=[("On", qb), ("sms", qb)])
                            for qb in range(4):
                                s2 = 28 + qb
                                P.op(ACT, lambda e, s2=s2: e.activation(out=small[:, s2:s2 + 1], in_=small[:, s2:s2 + 1], func=AF.Sqrt, bias=epsc[:], scale=1.0 / 256.0),
                                     reads=[("sms", qb), "epsc"], writes=[("sms", qb)])
                            for qb in range(4):
                                s2 = 28 + qb
                                P.op(DVE, lambda e, s2=s2: e.reciprocal(out=small[:, s2:s2 + 1], in_=small[:, s2:s2 + 1]), reads=[("sms", qb)], writes=[("sms", qb)])
                            for qb in range(4):
                                s2 = 28 + qb
                                P.op(DVE, lambda e, qb=qb, s2=s2: e.scalar_tensor_tensor(out=On[qb][:], in0=Od[qb][:], scalar=small[:, s2:s2 + 1], in1=sgb[:, j, :], op0=ALU.mult, op1=ALU.mult),
                                     reads=[("Od", qb), ("sms", qb), "sgb"], writes=[("On", qb)])

                        def tr_out(hd=hd):
                            for qb in range(4):
                                for hf in range(2):
                                    b = ps_next()
                                    P.op(PE, lambda e, b=b, qb=qb, hf=hf: e.transpose(ps[:, b, 0:128], On[qb][:, hf * 128:(hf + 1) * 128], ident[:]),
                                         reads=[("On", qb), "ident"], writes=[("ps", b)], signal=True)
                                    ci = 2 * hd + hf
                                    P.op(DVE, lambda e, b=b, ci=ci, qb=qb: e.tensor_copy(out=oT[:, ci, qb * 128:(qb + 1) * 128], in_=ps[:, b, 0:128]),
                                         reads=[("ps", b)], writes=[("oT", ci)])
                        def_b.append(phase_b)
                        def_t.append(tr_out)
            if t == 0 and l == layers[0]:
                dump("neglam", neglam[:], ["neglam"], [128, 2], F32)
            for f_ in def_b + def_t:
                f_()
            def_b.clear()
            def_t.clear()
            ps_mod[0] = 8
            for g in range(8):
                slot, skey = w_next("w_o", j * 8 + g)
                for o in range(2):
                    b = proj_fm(slot, skey, o, lambda kc: oT[:, kc, :], lambda kc: ("oT", kc))
                    resid_add(2 * g + o, b)

        for t in range(ntiles):
            P.barrier()
            tk_ = P.dma(SP, "xin", lambda e, t=t: e.dma_start(out=xin[:], in_=x_d[t * T:(t + 1) * T, :].rearrange("(a p) d -> p a d", p=128)),
                        writes=["xin"])
            P.region_dma.append(tk_)
            for c in range(KC):
                b = ps_next()
                for tb in range(4):
                    P.op(PE, lambda e, b=b, tb=tb, c=c: e.transpose(ps[:, b, tb * 128:(tb + 1) * 128], xin[:, tb, c * 128:(c + 1) * 128], ident[:]),
                         reads=["xin", "ident"], writes=[("ps", b)], signal=(tb == 3))
                if evac_engine() == ACT:
                    P.op(ACT, lambda e, b=b, c=c: e.activation(out=x[:, c, :], in_=ps[:, b, :], func=AF.Copy), reads=[("ps", b)], writes=[("x", c)])
                else:
                    P.op(DVE, lambda e, b=b, c=c: e.tensor_copy(out=x[:, c, :], in_=ps[:, b, :]), reads=[("ps", b)], writes=[("x", c)])
            for l in layers:
                if not DEBUG["mixer"]:
                    pass
                elif l % 2 == 0:
                    even_mixer(l)
                else:
                    odd_mixer(l, t)
                if t == 0 and l == layers[0]:
                    dump("xmix", x[:], [("x", c) for c in range(KC)], [128, KC, T], F32)
                if DEBUG["ffn"]:
                    ffn(l, t)
            P.barrier()
            if final:
                rmsnorm(C_FING, xn, "xn")
                src, skey_of = xn, (lambda c: ("xn", c))
            else:
                src, skey_of = x, (lambda c: ("x", c))
            for tb in range(4):
                for cg in range(4):
                    b = ps_next()
                    for cc in range(4):
                        c = cg * 4 + cc
                        P.op(PE, lambda e, b=b, tb=tb, c=c, cc=cc: e.transpose(ps[:, b, cc * 128:(cc + 1) * 128], src[:, c, tb * 128:(tb + 1) * 128], ident[:]),
                             reads=[skey_of(c), "ident"], writes=[("ps", b)], signal=(cc == 3))
                    if tb % 2 == 0:
                        P.op(ACT, lambda e, b=b, tb=tb, cg=cg: e.activation(out=yout[:, tb, cg * 512:(cg + 1) * 512], in_=ps[:, b, :], func=AF.Copy),
                             reads=[("ps", b)], writes=[("yout", tb)])
                    else:
                        P.op(DVE, lambda e, b=b, tb=tb, cg=cg: e.tensor_copy(out=yout[:, tb, cg * 512:(cg + 1) * 512], in_=ps[:, b, :]),
                             reads=[("ps", b)], writes=[("yout", tb)])
            tk_ = P.dma(SP, "yst", lambda e, t=t: e.dma_start(out=y_d[t * T:(t + 1) * T, :].rearrange("(a p) d -> p a d", p=128), in_=yout[:]),
                        reads=[("yout", tb) for tb in range(4)], writes=[("ydram", t)])
            P.region_dma.append(tk_)
        P.barrier()
        P.emit(blk)
    return nc


def _groupify(W, col_lists):
    K = W.shape[0]
    kc = K // 128
    out = np.empty((len(col_lists), 128, kc * len(col_lists[0])), np.float32)
    for g, cols in enumerate(col_lists):
        Wg = W[:, cols].reshape(kc, 128, len(cols)).transpose(1, 0, 2)
        out[g] = Wg.reshape(128, -1)
    return out


def _fm(v):
    v = np.asarray(v, np.float32)
    lead = v.shape[:-1]
    n = v.shape[-1] // 128
    return np.moveaxis(v.reshape(lead + (n, 128)), -1, 0)


def prepare_consts(inp):
    r = np.arange
    w_in, w_out, w_qkv, w_o, w_up, w_down = [], [], [], [], [], []
    for j in range(2):
        W = inp["ev_w_in"][j]
        cl = [np.concatenate([r(i * 128, (i + 1) * 128), 1024 + r(i * 128, (i + 1) * 128)]) for i in range(8)]
        cl += [3072 + r(vg * 256, (vg + 1) * 256) for vg in range(4)]
        cl += [2048 + r(ug * 256, (ug + 1) * 256) for ug in range(4)]
        w_in.append(_groupify(W, cl))
        w_out.append(_groupify(inp["ev_w_out"][j], [r(g * 256, (g + 1) * 256) for g in range(8)]))
        w_qkv.append(_groupify(inp["od_w_qkv"][j], [r(g * 256, (g + 1) * 256) for g in range(24)]))
        w_o.append(_groupify(inp["od_w_o"][j], [r(g * 256, (g + 1) * 256) for g in range(8)]))
    for l in range(4):
        W = inp["ffn_w_up"][l]
        w_up.append(_groupify(W, [np.concatenate([r(g * 128, (g + 1) * 128), DFF + r(g * 128, (g + 1) * 128)]) for g in range(NFF)]))
        Wd = inp["ffn_w_down"][l]
        gd = np.empty((32, 128, 2816), np.float32)
        for m in range(16):
            for half in range(2):
                blk = Wd[half * 2816:(half + 1) * 2816, m * 128:(m + 1) * 128].reshape(22, 128, 128).transpose(1, 0, 2)
                gd[m * 2 + half] = blk.reshape(128, -1)
        w_down.append(gd)
    cfm = np.zeros((128, NFM), np.float32)
    cfm[:, C_MIXG:C_MIXG + 64] = _fm(inp["norm_mix_g"]).reshape(128, 64)
    cfm[:, C_FFNG:C_FFNG + 64] = _fm(inp["norm_ffn_g"]).reshape(128, 64)
    cfm[:, C_FING:C_FING + 16] = _fm(inp["final_norm_g"]).reshape(128, 16)
    cw = _fm(inp["ev_conv_w"])
    cfm[:, C_CW:C_CW + 496] = cw.transpose(0, 1, 3, 2).reshape(128, 496)
    cfm[:, C_CB:C_CB + 16] = _fm(inp["ev_conv_b"]).reshape(128, 16)
    cfm[:, C_LAG:C_LAG + 16] = _fm(inp["ev_ln_a_g"]).reshape(128, 16)
    cfm[:, C_LAB:C_LAB + 16] = _fm(inp["ev_ln_a_b"]).reshape(128, 16)
    fw = _fm(inp["ffn_conv_w"])
    cfm[:, C_FW:C_FW + 1056] = fw.transpose(0, 1, 3, 2).reshape(128, 1056)
    cfm[:, C_FB:C_FB + 352] = _fm(inp["ffn_conv_b"]).reshape(128, 352)
    cbc = np.zeros((1, NBC), np.float32)
    for j in range(2):
        o = B_EV + j * 3072
        cbc[0, o:o + 1024] = inp["ev_ln_v_g"][j]
        cbc[0, o + 1024:o + 2048] = inp["ev_ln_v_b"][j]
        cbc[0, o + 2048:o + 3072] = np.asarray(inp["ev_b_s"][j]).reshape(-1)
    cbc[0, B_SUB:B_SUB + 512] = np.asarray(inp["od_subln_g"]).reshape(-1)
    for w, name in enumerate(["od_lambda_q1", "od_lambda_k1", "od_lambda_q2", "od_lambda_k2"]):
        cbc[0, B_LAM + w * 256:B_LAM + (w + 1) * 256] = np.asarray(inp[name]).reshape(-1)
    wst = np.ascontiguousarray(np.asarray(inp["ev_w_s"], np.float32).transpose(0, 1, 3, 2)).reshape(16, 128, 128)
    idm = np.stack([np.eye(128, dtype=np.float32), np.triu(np.ones((128, 128), np.float32))])
    return {
        "w_in": np.concatenate(w_in), "w_out": np.concatenate(w_out), "w_qkv": np.concatenate(w_qkv),
        "w_o": np.concatenate(w_o), "w_up": np.concatenate(w_up), "w_down": np.concatenate(w_down),
        "cfm": cfm, "cbc": cbc, "wst": wst, "idm": idm,
    }


LAUNCH_PLAN = [([0, 1, 2, 3], True)]


def run_plan(inp, plan, cores=8, ntiles=NT, trace=False):
    consts = prepare_consts(inp)
    xs = [np.ascontiguousarray(np.asarray(inp["x"][b], np.float32)) for b in range(cores)]
    res = None
    for layers, final in plan:
        nc = build_program(layers, final, ntiles)
        in_maps = [dict(consts, x=xs[b]) for b in range(cores)]
        res = run_bass_kernel_spmd(nc, in_maps, core_ids=list(range(cores)), trace=trace)
        xs = [np.asarray(res.results[b]["y"], np.float32) for b in range(cores)]
    return xs, res


LAST_RES = None


def kernel(**inputs):
    xs, _ = run_plan(inputs, LAUNCH_PLAN, cores=8)
    return np.stack(xs, axis=0).astype(np.float32)
```

```python
import contextlib
import math
import numpy as np
import ml_dtypes
import concourse.bass as bass
import concourse.mybir as mybir
from concourse.bass_utils import run_bass_kernel_spmd

F32 = mybir.dt.float32
BF16 = mybir.dt.bfloat16
AF = mybir.ActivationFunctionType
ALU = mybir.AluOpType
AX = mybir.AxisListType

PE, ACT, DVE, POOL, SP = "pe", "act", "dve", "pool", "sp"
ENGS = (PE, ACT, DVE, POOL, SP)

S = 4096
D = 2048
T = 512
NT = S // T
KC = 16
DFF = 5632
NFF = DFF // 128
EPS = 1e-6
SCALE = 1.0 / math.sqrt(128.0)
NSLOT = 5
NKV = 4

C_MIXG, C_FFNG, C_FING, C_CW, C_CB, C_LAG, C_LAB, C_FW, C_FB, NFM = 0, 64, 128, 144, 640, 656, 672, 688, 1744, 2096
B_EV, B_SUB, B_LAM, NBC = 0, 6144, 6656, 7680


class Prog:
    def __init__(self, nc, stack):
        self.nc = nc
        self.stack = stack
        self.ops = {e: [] for e in ENGS}
        self.esem = {e: stack.enter_context(nc.semaphore("prog_" + e)) for e in (PE, ACT, DVE, POOL)}
        self.cnt = {e: 0 for e in (PE, ACT, DVE, POOL)}
        self.dsem = {}
        self.dcnt = {}
        self.waited = {e: {} for e in ENGS}
        self.last_w = {}
        self.readers = {}
        self.region_dma = []
        self.sim = {e: [] for e in ENGS}

    def dma_sem(self, name):
        if name not in self.dsem:
            self.dsem[name] = self.stack.enter_context(self.nc.semaphore("dma_" + name))
            self.dcnt[name] = 0
        return self.dsem[name]

    def wait(self, eng, tok):
        if tok is None:
            return
        kind, name, val = tok
        if kind == "e" and name == PE and eng == PE:
            return
        key = (kind, name)
        if self.waited[eng].get(key, 0) >= val:
            return
        self.waited[eng][key] = val
        sem = self.esem[name] if kind == "e" else self.dsem[name]
        self.ops[eng].append(lambda e, sem=sem, val=val: e.wait_ge(sem, val))
        self.sim[eng].append(("w", key, val))

    def _deps(self, eng, reads, writes, extra):
        for k in reads:
            self.wait(eng, self.last_w.get(k))
        for k in writes:
            self.wait(eng, self.last_w.get(k))
            for t in self.readers.get(k, ()):
                self.wait(eng, t)
        for t in extra:
            self.wait(eng, t)

    def _commit(self, tok, reads, writes):
        for k in reads:
            self.readers.setdefault(k, []).append(tok)
        for k in writes:
            self.last_w[k] = tok
            self.readers[k] = []

    def op(self, eng, fn, reads=(), writes=(), signal=True, extra=()):
        self._deps(eng, reads, writes, extra)
        if signal:
            self.cnt[eng] += 1
            tok = ("e", eng, self.cnt[eng])
            sem = self.esem[eng]
            self.ops[eng].append(lambda e, fn=fn, sem=sem: fn(e).then_inc(sem, 1))
            self.sim[eng].append(("i", ("e", eng), 1))
        else:
            tok = ("e", eng, self.cnt[eng] + 1)
            self.ops[eng].append(lambda e, fn=fn: fn(e))
        self._commit(tok, reads, writes)
        return tok

    def dma(self, q, semname, fn, reads=(), writes=(), extra=(), region=False):
        self._deps(q, reads, writes, extra)
        sem = self.dma_sem(semname)
        self.dcnt[semname] += 16
        tok = ("d", semname, self.dcnt[semname])
        self.ops[q].append(lambda e, fn=fn, sem=sem: fn(e).then_inc(sem, 16))
        self.sim[q].append(("i", ("d", semname), 16))
        self._commit(tok, reads, writes)
        if region:
            self.region_dma.append(tok)
        return tok

    def last_tok(self, eng):
        return ("e", eng, self.cnt[eng]) if self.cnt[eng] else None

    def barrier(self):
        toks = [self.last_tok(e) for e in (PE, ACT, DVE, POOL)] + list(self.region_dma)
        self.region_dma = []
        for eng in (PE, ACT, DVE, SP):
            for t in toks:
                self.wait(eng, t)

    def check_deadlock(self):
        pos = {e: 0 for e in ENGS}
        val = {}
        progress = True
        while progress:
            progress = False
            for e in ENGS:
                lst = self.sim[e]
                while pos[e] < len(lst):
                    k, key, v = lst[pos[e]]
                    if k == "w":
                        if val.get(key, 0) < v:
                            break
                    else:
                        val[key] = val.get(key, 0) + v
                    pos[e] += 1
                    progress = True
        stuck = {e: (pos[e], len(self.sim[e]), self.sim[e][pos[e]]) for e in ENGS if pos[e] < len(self.sim[e])}
        if stuck:
            raise RuntimeError("DEADLOCK in recorded program: %r" % (stuck,))

    def emit(self, block):
        self.check_deadlock()
        prog = self

        @block.tensor
        def _(e):
            for f in prog.ops[PE]:
                f(e)

        @block.scalar
        def _(e):
            for f in prog.ops[ACT]:
                f(e)

        @block.vector
        def _(e):
            for f in prog.ops[DVE]:
                f(e)

        @block.gpsimd
        def _(e):
            for f in prog.ops[POOL]:
                f(e)

        @block.sync
        def _(e):
            for f in prog.ops[SP]:
                f(e)


def lambda_init(i):
    return 0.8 - 0.6 * math.exp(-0.3 * i)


DEBUG = {"mixer": True, "ffn": True, "dump": False}


def weight_schedule(layers):
    seq = []
    for l in layers:
        j = l // 2
        if not DEBUG["mixer"]:
            pass
        elif l % 2 == 0:
            seq += [("w_in", j * 16 + g, 4096) for g in range(16)]
            seq += [("w_dg", j * 8 + g, 4096) for g in range(8)]
            seq += [("w_out", j * 8 + g, 4096) for g in range(8)]
        else:
            seq += [("w_qkv", j * 24 + g, 4096) for g in range(24)]
            seq += [("w_o", j * 8 + g, 4096) for g in range(8)]
        if DEBUG["ffn"]:
            seq += [("w_up", l * 44 + g, 4096) for g in range(44)]
            seq += [("w_down", l * 32 + (mq + mi) * 2 + half, 2816) for mq in range(0, 16, 4) for half in range(2) for mi in range(4)]
    return seq


def build_program(layers, final, ntiles=NT):
    nc = bass.Bass("TRN2", target_bir_lowering=False)
    dt = nc.dram_tensor
    x_d = dt("x", [S, D], F32, kind="ExternalInput").ap()
    y_d = dt("y", [S, D], F32, kind="ExternalOutput").ap()
    wshape = {"w_in": (32, 4096), "w_out": (16, 4096), "w_qkv": (48, 4096), "w_o": (16, 4096),
              "w_up": (176, 4096), "w_down": (128, 2816)}
    w_d = {k: dt(k, [g, 128, e], F32, kind="ExternalInput").ap() for k, (g, e) in wshape.items()}
    wb_d = {k: dt("b_" + k, [g, 128, e], BF16, kind="Internal").ap() for k, (g, e) in wshape.items()}
    wb_d["w_dg"] = dt("b_w_dg", [16, 128, 4096], BF16, kind="Internal").ap()
    cfm_d = dt("cfm", [128, NFM], F32, kind="ExternalInput").ap()
    cbc_d = dt("cbc", [1, NBC], F32, kind="ExternalInput").ap()
    wst_d = dt("wst", [16, 128, 128], F32, kind="ExternalInput").ap()
    idm_d = dt("idm", [2, 128, 128], F32, kind="ExternalInput").ap()
    kc_d = dt("kcache", [2, 16, NT, 128, T], BF16, kind="Internal").ap()
    vc_d = dt("vcache", [2, 8, NT, 128, 4 * 260], BF16, kind="Internal").ap()

    with contextlib.ExitStack() as st:
        P = Prog(nc, st)

        def sb(name, shape, dtype):
            return st.enter_context(nc.sbuf_tensor("sb_" + name, shape, dtype))

        x = sb("x", [128, KC, T], F32)
        h = sb("h", [128, KC, T], BF16)
        wslot = [sb("wslot%d" % i, [128, 4096], BF16) for i in range(NSLOT)]
        cfm = sb("cfm_sb", [128, NFM], F32)
        wsm = sb("wsm", [128, 16, 128], BF16)
        ident = sb("ident", [128, 128], F32)
        maskf = sb("maskf", [128, 128], F32)
        maskb = sb("maskb", [128, 128], BF16)
        identb = sb("identb", [128, 128], BF16)
        onesb = sb("onesb", [128, 128], BF16)
        onesf = sb("onesf", [128, 128], F32)
        epsc = sb("epsc", [128, 1], F32)
        sgb = sb("sgb", [128, 2, 256], F32)
        lamw = sb("lamw", [128, 8], F32)
        neglam = sb("neglam", [128, 2], F32)
        halo_f = sb("halo_f", [128, 4 * 88, 2], F32)
        halo_a = sb("halo_a", [128, 2, 8, 30], BF16)
        rstd = sb("rstd", [128, T], F32)
        sqb = [sb("sqb%d" % i, [128, T], BF16) for i in range(2)]
        small = sb("small", [128, 64], F32)
        R = sb("R", [128, 94 * 1024], mybir.dt.uint8)

        def rview(off_kib, shape, dtype):
            n = int(np.prod(shape[1:]))
            esz = 2 if dtype == BF16 else 4
            off = int(off_kib * 1024)
            v = R[:, off:off + n * esz].bitcast(dtype)
            if len(shape) == 3:
                v = v.rearrange("p (a b) -> p a b", a=shape[1])
            elif len(shape) == 4:
                v = v.rearrange("p (a b c) -> p a b c", a=shape[1], b=shape[2])
            return v

        wsf = rview(0, [128, 16, 128], F32)
        lamt = rview(8, [128, 8, 128], F32)
        G = rview(0, [128, NFF, T], BF16)
        zb = [[rview(44 + 4.25 * (2 * b + s_), [128, 520], F32) for s_ in range(2)] for b in range(2)]
        t1 = [[rview(61 + 2 * (2 * b + s_), [128, T], F32) for s_ in range(2)] for b in range(2)]
        qT = rview(0, [128, 16, T], BF16)
        kT = rview(16, [128, 16, T], BF16)
        vcur = rview(32, [128, 4, 8, 260], BF16)
        kslot = [rview(48.25 + i, [128, T], BF16) for i in range(NKV)]
        vslot = [rview(52.25 + 2.25 * i, [128, 4, 260], BF16) for i in range(NKV)]
        PT = [rview(61.25 + i, [128, T], BF16) for i in range(4)]
        O1n = rview(65.25, [128, 4, 256], F32)
        Od = [rview(69.25 + i, [128, 256], F32) for i in range(4)]
        On = [rview(73.25 + i, [128, 256], F32) for i in range(4)]
        oT = rview(77.25, [128, 16, T], BF16)
        abuf = rview(0, [128, 8, 544], BF16)
        dgst = [rview(16 + 8 * i, [128, 32, 128], BF16) for i in range(2)]
        evbc = rview(8.5, [128, 3072], F32)
        ybuf = rview(20.5, [128, 8, T], F32)
        vtok = rview(36.5, [128, 4, 1024], F32)
        vnb = rview(52.5, [128, 4, 1024], BF16)
        AB = rview(60.5, [128, 16, T], BF16)
        wk = [rview(76.5 + 2 * i, [128, T], F32) for i in range(5)]
        xin = rview(0, [128, 4, D], F32)
        xn = rview(32, [128, KC, T], F32)
        yout = rview(32, [128, 4, D], F32)

        ps = st.enter_context(nc.psum_tensor("ps", [128, 8, 512], F32))
        blk = st.enter_context(nc.Block())

        def dump(name, ap, keys, shape, dtype):
            if not DEBUG["dump"]:
                return
            d = nc.dram_tensor("dbg_" + name, shape, dtype, kind="ExternalOutput").ap()
            P.dma(SP, "dbg", lambda e: e.dma_start(out=d, in_=ap), reads=keys, region=True)

        ps_rr = [0]

        ps_mod = [8]

        def ps_next():
            i = ps_rr[0] % ps_mod[0]
            ps_rr[0] += 1
            return i

        cp_rr = [0]

        def evac_engine():
            cp_rr[0] += 1
            return ACT if cp_rr[0] % 2 else DVE

        P.dma(SP, "c0", lambda e: e.dma_start(out=cfm[:], in_=cfm_d), writes=["cfm"], region=True)
        P.dma(SP, "c1", lambda e: e.dma_start(out=ident[:], in_=idm_d[0]), writes=["ident"], region=True)
        P.dma(SP, "c2", lambda e: e.dma_start(out=maskf[:], in_=idm_d[1]), writes=["maskf"], region=True)
        P.dma(SP, "c3", lambda e: e.dma_start(out=wsf[:], in_=wst_d.rearrange("g s t -> s g t")), writes=["wsf"], region=True)
        P.dma(SP, "c4", lambda e: e.dma_start(out=sgb[:].rearrange("p a b -> p (a b)"),
                                              in_=cbc_d[:, B_SUB:B_SUB + 512].partition_broadcast(128)), writes=["sgb"], region=True)
        P.dma(SP, "c5", lambda e: e.dma_start(out=lamt[:].rearrange("p a b -> p (a b)"),
                                              in_=cbc_d[:, B_LAM:B_LAM + 1024].partition_broadcast(128)), writes=["lamt"], region=True)
        P.op(DVE, lambda e: e.memset(onesb[:], 1.0 / 2048.0), writes=["onesb"])
        P.op(DVE, lambda e: e.memset(onesf[:], 1.0 / 1024.0), writes=["onesf"])
        P.op(DVE, lambda e: e.memset(epsc[:], EPS), writes=["epsc"])
        P.op(DVE, lambda e: e.memset(halo_f[:], 0.0), writes=["halo_f"])
        P.op(DVE, lambda e: e.memset(halo_a[:], 0.0), writes=["halo_a"])
        P.op(DVE, lambda e: e.tensor_copy(out=maskb[:], in_=maskf[:]), reads=["maskf"], writes=["maskb"])
        P.op(DVE, lambda e: e.tensor_copy(out=identb[:], in_=ident[:]), reads=["ident"], writes=["identb"])
        if DEBUG["mixer"]:
            for g in sorted({l // 2 * 8 + i for l in layers if l % 2 == 0 for i in range(8)}):
                stg = dgst[g % 2]
                P.op(DVE, lambda e, stg=stg: e.memset(stg[:, 31, :], 0.0), writes=[("dgst", g % 2)])
                for k in range(31):
                    wc = C_CW + g * 31 + k
                    P.op(DVE, lambda e, stg=stg, k=k, wc=wc: e.tensor_scalar(out=stg[:, k, :], in0=identb[:], scalar1=cfm[:, wc:wc + 1], scalar2=None, op0=ALU.mult),
                         reads=["identb", "cfm"], writes=[("dgst", g % 2)])
                P.dma(SP, "dgw", lambda e, stg=stg, g=g: e.dma_start(out=wb_d["w_dg"][g], in_=stg[:].rearrange("p a b -> p (a b)")),
                      reads=[("dgst", g % 2)], writes=[("wbg", g)], region=True)
        for g in range(16):
            P.op(DVE, lambda e, g=g: e.tensor_tensor(out=wsm[:, g, :], in0=wsf[:, g, :], in1=maskf[:], op=ALU.mult),
                 reads=["wsf", "maskf"], writes=["wsm"])
        for j in range(2):
            li = lambda_init(2 * j + 1)
            for a, (qi, ki) in enumerate(((0 + j, 2 + j), (4 + j, 6 + j))):
                P.op(DVE, lambda e, qi=qi, ki=ki: e.tensor_tensor(out=lamt[:, qi, :], in0=lamt[:, qi, :], in1=lamt[:, ki, :], op=ALU.mult),
                     reads=["lamt"], writes=["lamt"])
                P.op(DVE, lambda e, qi=qi, c=2 * j + a: e.reduce_sum(out=lamw[:, c:c + 1], in_=lamt[:, qi, :], axis=AX.X),
                     reads=["lamt"], writes=["lamw"])
                P.op(ACT, lambda e, c=2 * j + a: e.activation(out=lamw[:, c:c + 1], in_=lamw[:, c:c + 1], func=AF.Exp),
                     reads=["lamw"], writes=["lamw"])
            P.op(DVE, lambda e, j=j: e.tensor_tensor(out=neglam[:, j:j + 1], in0=lamw[:, 2 * j + 1:2 * j + 2], in1=lamw[:, 2 * j:2 * j + 1], op=ALU.subtract),
                 reads=["lamw"], writes=["neglam"])
            P.op(DVE, lambda e, j=j, li=li: e.tensor_scalar(out=neglam[:, j:j + 1], in0=neglam[:, j:j + 1], scalar1=-li, scalar2=None, op0=ALU.add),
                 reads=["neglam"], writes=["neglam"])
            P.op(DVE, lambda e, j=j, li=li: e.tensor_scalar(out=sgb[:, j, :], in0=sgb[:, j, :], scalar1=1.0 - li, scalar2=None, op0=ALU.mult),
                 reads=["sgb"], writes=["sgb"])

        sched = weight_schedule(layers)
        full = sched * ntiles
        wpos = [0]
        wissued = [0]

        def w_issue():
            i = wissued[0]
            if i >= len(full):
                return
            kind, g, e_ = full[i]
            s_ = i % NSLOT
            if i < len(sched) and kind != "w_dg":
                P.dma(POOL, "wc%d" % s_, lambda e, kind=kind, g=g, e_=e_, s_=s_: e.dma_start(out=wslot[s_][:, 0:e_], in_=w_d[kind][g]),
                      writes=[("ws", s_)])
                if ntiles > 1:
                    P.dma(SP, "wk%d" % s_, lambda e, kind=kind, g=g, e_=e_, s_=s_: e.dma_start(out=wb_d[kind][g], in_=wslot[s_][:, 0:e_]),
                          reads=[("ws", s_)], writes=[("wbg", kind, g)])
            else:
                P.dma(SP, "w%d" % s_, lambda e, kind=kind, g=g, e_=e_, s_=s_: e.dma_start(out=wslot[s_][:, 0:e_], in_=wb_d[kind][g]),
                      reads=[("wbg", g) if kind == "w_dg" else ("wbg", kind, g)], writes=[("ws", s_)])
            wissued[0] += 1

        def w_next(kind, g):
            i = wpos[0]
            assert full[i][0] == kind and full[i][1] == g, (full[i], kind, g)
            while wissued[0] < min(i + NSLOT, len(full)):
                w_issue()
            wpos[0] += 1
            s_ = i % NSLOT
            return wslot[s_], ("ws", s_)

        def mm_group(out_ap, out_key, pairs):
            n = len(pairs)
            tok = None
            for i, (l_, r_, rk) in enumerate(pairs):
                tok = P.op(PE, lambda e, l_=l_, r_=r_, i=i: e.matmul(out_ap, l_, r_, start=(i == 0), stop=(i == n - 1)),
                           reads=rk, writes=[out_key], signal=(i == n - 1))
            return tok

        def rmsnorm(gcol, out, out_key):
            b = ps_next()
            for c in range(KC):
                sq = sqb[c % 2]
                P.op(ACT, lambda e, c=c, sq=sq: e.activation(out=sq[:], in_=x[:, c, :], func=AF.Square),
                     reads=[("x", c)], writes=[("sqb", c % 2)])
                P.op(PE, lambda e, c=c, sq=sq, b=b: e.matmul(ps[:, b, :], onesb[:], sq[:], start=(c == 0), stop=(c == KC - 1)),
                     reads=[("sqb", c % 2), "onesb"], writes=[("ps", b)], signal=True)
            P.op(ACT, lambda e, b=b: e.activation(out=rstd[:], in_=ps[:, b, :], func=AF.Sqrt, bias=epsc[:], scale=1.0),
                 reads=[("ps", b), "epsc"], writes=["rstd"])
            P.op(DVE, lambda e: e.reciprocal(out=rstd[:], in_=rstd[:]), reads=["rstd"], writes=["rstd"])
            for c in range(KC):
                P.op(DVE, lambda e, c=c: e.scalar_tensor_tensor(out=out[:, c, :], in0=x[:, c, :], scalar=cfm[:, gcol + c:gcol + c + 1],
                                                                in1=rstd[:], op0=ALU.mult, op1=ALU.mult),
                     reads=[("x", c), "rstd", "cfm"], writes=[(out_key, c)])

        def proj_fm(slot, skey, o, rhs_of, rhs_key_of, nk=KC, width=256):
            b = ps_next()
            pairs = [(slot[:, kc * width + o * 128: kc * width + (o + 1) * 128], rhs_of(kc), [skey, rhs_key_of(kc)]) for kc in range(nk)]
            mm_group(ps[:, b, :], ("ps", b), pairs)
            return b

        def resid_add(m, b):
            P.op(DVE, lambda e, m=m, b=b: e.tensor_tensor(out=x[:, m, :], in0=x[:, m, :], in1=ps[:, b, :], op=ALU.add),
                 reads=[("ps", b), ("x", m)], writes=[("x", m)])

        def ffn(l, t=1):
            P.barrier()
            rmsnorm(C_FFNG + l * 16, h, "h")
            if t == 0 and l == layers[0]:
                dump("hffn", h[:], [("h", c) for c in range(KC)], [128, KC, T], BF16)
            for j in range(NFF):
                slot, skey = w_next("w_up", l * 44 + j)
                bb = j % 2
                banks = []
                for s_ in range(2):
                    b = proj_fm(slot, skey, s_, lambda kc: h[:, kc, :], lambda kc: ("h", kc))
                    banks.append(b)
                for s_ in range(2):
                    b = banks[s_]
                    jj = j + 44 * s_
                    z = zb[bb][s_]
                    tt = t1[bb][s_]
                    zk, tk = ("zb", bb, s_), ("t1", bb, s_)
                    hcol = l * 88 + jj
                    wc = C_FW + hcol * 3
                    bc = C_FB + hcol
                    P.op(DVE, lambda e, z=z, hcol=hcol: e.tensor_copy(out=z[:, 0:2], in_=halo_f[:, hcol, :]),
                         reads=[("halo_f", hcol)], writes=[zk])
                    P.op(ACT, lambda e, z=z, b=b: e.activation(out=z[:, 2:2 + T], in_=ps[:, b, :], func=AF.Copy),
                         reads=[("ps", b)], writes=[zk])
                    P.op(ACT, lambda e, tt=tt, b=b, wc=wc, bc=bc: e.activation(out=tt[:], in_=ps[:, b, :], func=AF.Identity,
                                                                              scale=cfm[:, wc + 2:wc + 3], bias=cfm[:, bc:bc + 1]),
                         reads=[("ps", b), "cfm"], writes=[tk])
                    P.op(DVE, lambda e, z=z, hcol=hcol: e.tensor_copy(out=halo_f[:, hcol, :], in_=z[:, T:T + 2]),
                         reads=[zk], writes=[("halo_f", hcol)])
                    P.op(DVE, lambda e, z=z, tt=tt, wc=wc: e.scalar_tensor_tensor(out=tt[:], in0=z[:, 1:1 + T], scalar=cfm[:, wc + 1:wc + 2],
                                                                               in1=tt[:], op0=ALU.mult, op1=ALU.add),
                         reads=[zk, tk, "cfm"], writes=[tk])
                    P.op(DVE, lambda e, z=z, tt=tt, wc=wc: e.scalar_tensor_tensor(out=tt[:], in0=z[:, 0:T], scalar=cfm[:, wc:wc + 1],
                                                                               in1=tt[:], op0=ALU.mult, op1=ALU.add),
                         reads=[zk, tk, "cfm"], writes=[tk])
                tg, tv = t1[bb][0], t1[bb][1]
                P.op(ACT, lambda e, tg=tg: e.activation(out=tg[:], in_=tg[:], func=AF.Silu),
                     reads=[("t1", bb, 0)], writes=[("t1", bb, 0)])
                P.op(DVE, lambda e, tg=tg, tv=tv, j=j: e.tensor_tensor(out=G[:, j, :], in0=tg[:], in1=tv[:], op=ALU.mult),
                     reads=[("t1", bb, 0), ("t1", bb, 1)], writes=[("G", j)])
            if t == 0 and l == layers[0]:
                dump("G", G, [("G", c) for c in range(NFF)], [128, NFF, T], BF16)
            for mq in range(0, KC, 4):
                banks = [ps_next() for _ in range(4)]
                for half in range(2):
                    for mi in range(4):
                        m, b = mq + mi, banks[mi]
                        slot, skey = w_next("w_down", l * 32 + m * 2 + half)
                        for kk in range(22):
                            kc = half * 22 + kk
                            P.op(PE, lambda e, b=b, slot=slot, kk=kk, kc=kc: e.matmul(ps[:, b, :], slot[:, kk * 128:(kk + 1) * 128], G[:, kc, :],
                                                                                     start=(kc == 0), stop=(kc == 43)),
                                 reads=[skey, ("G", kc)], writes=[("ps", b)], signal=(kc == 43))
                        if half == 1:
                            resid_add(m, b)

        def even_mixer(l):
            j = l // 2
            P.barrier()
            tok = P.dma(SP, "evbc", lambda e: e.dma_start(out=evbc[:], in_=cbc_d[:, B_EV + j * 3072:B_EV + (j + 1) * 3072].partition_broadcast(128)),
                        writes=["evbc"], region=True)
            rmsnorm(C_MIXG + l * 16, h, "h")
            P.op(DVE, lambda e: e.tensor_copy(out=abuf[:, :, 0:30], in_=halo_a[:, j, :, :]), reads=["halo_a"], writes=[("abuf", i) for i in range(8)])
            for i in range(8):
                slot, skey = w_next("w_in", j * 16 + i)
                bv = proj_fm(slot, skey, 0, lambda kc: h[:, kc, :], lambda kc: ("h", kc))
                bg = proj_fm(slot, skey, 1, lambda kc: h[:, kc, :], lambda kc: ("h", kc))
                w_ = wk[i % 2]
                P.op(ACT, lambda e, w_=w_, bg=bg: e.activation(out=w_[:], in_=ps[:, bg, :], func=AF.Sigmoid),
                     reads=[("ps", bg)], writes=[("wk", i % 2)])
                P.op(DVE, lambda e, w_=w_, bv=bv, i=i: e.tensor_tensor(out=abuf[:, i, 30:30 + T], in0=ps[:, bv, :], in1=w_[:], op=ALU.mult),
                     reads=[("ps", bv), ("wk", i % 2)], writes=[("abuf", i)])
            P.op(DVE, lambda e: e.tensor_copy(out=halo_a[:, j, :, :], in_=abuf[:, :, T:T + 30]), reads=[("abuf", i) for i in range(8)], writes=["halo_a"])
            for vg in range(4):
                slot, skey = w_next("w_in", j * 16 + 8 + vg)
                for tb in range(4):
                    b = ps_next()
                    pairs = [(h[:, kc, tb * 128:(tb + 1) * 128], slot[:, kc * 256:(kc + 1) * 256], [skey, ("h", kc)]) for kc in range(KC)]
                    mm_group(ps[:, b, 0:256], ("ps", b), pairs)
                    P.op(ACT, lambda e, b=b, tb=tb, vg=vg: e.activation(out=vtok[:, tb, vg * 256:(vg + 1) * 256], in_=ps[:, b, 0:256], func=AF.Gelu_apprx_tanh),
                         reads=[("ps", b)], writes=[("vtok", tb, vg)])
            for tb in range(4):
                vk = [("vtok", tb, vg) for vg in range(4)]
                so = tb * 16
                for hh in range(2):
                    P.op(DVE, lambda e, tb=tb, hh=hh, so=so: e.bn_stats(out=small[:, so + hh * 6:so + hh * 6 + 6], in_=vtok[:, tb, hh * 512:(hh + 1) * 512]),
                         reads=vk, writes=[("small", tb)])
                P.op(DVE, lambda e, so=so: e.bn_aggr(out=small[:, so + 12:so + 14], in_=small[:, so:so + 12]), reads=[("small", tb)], writes=[("small", tb)])
                P.op(ACT, lambda e, so=so: e.activation(out=small[:, so + 14:so + 15], in_=small[:, so + 13:so + 14], func=AF.Sqrt, bias=epsc[:], scale=1.0),
                     reads=[("small", tb), "epsc"], writes=[("small", tb)])
                P.op(DVE, lambda e, so=so: e.reciprocal(out=small[:, so + 14:so + 15], in_=small[:, so + 14:so + 15]), reads=[("small", tb)], writes=[("small", tb)])
                P.op(DVE, lambda e, tb=tb, so=so: e.tensor_scalar(out=vtok[:, tb, :], in0=vtok[:, tb, :], scalar1=small[:, so + 12:so + 13],
                                                                 scalar2=small[:, so + 14:so + 15], op0=ALU.subtract, op1=ALU.mult),
                     reads=vk + [("small", tb)], writes=vk)
                P.op(DVE, lambda e, tb=tb: e.tensor_tensor(out=vtok[:, tb, :], in0=vtok[:, tb, :], in1=evbc[:, 0:1024], op=ALU.mult),
                     reads=vk + ["evbc"], writes=vk)
                P.op(DVE, lambda e, tb=tb: e.tensor_tensor(out=vnb[:, tb, :], in0=vtok[:, tb, :], in1=evbc[:, 1024:2048], op=ALU.add),
                     reads=vk + ["evbc"], writes=[("vnb", tb)])
            for ug in range(4):
                slot, skey = w_next("w_in", j * 16 + 12 + ug)
                for o in range(2):
                    b = proj_fm(slot, skey, o, lambda kc: h[:, kc, :], lambda kc: ("h", kc))
                    ci = 8 + 2 * ug + o
                    P.op(ACT, lambda e, b=b, ci=ci: e.activation(out=AB[:, ci, :], in_=ps[:, b, :], func=AF.Gelu_apprx_tanh),
                         reads=[("ps", b)], writes=[("AB", ci)])
            for hd in range(8):
                b = ps_next()
                for tb in range(4):
                    P.op(PE, lambda e, b=b, tb=tb, hd=hd: e.matmul(ps[:, b, tb * 128:(tb + 1) * 128], vnb[:, tb, hd * 128:(hd + 1) * 128],
                                                                   wsm[:, j * 8 + hd, :], start=True, stop=True),
                         reads=[("vnb", tb), "wsm"], writes=[("ps", b)], signal=(tb == 3))
                w_ = wk[2 + hd % 2]
                wkey = ("wk", 2 + hd % 2)
                for tb in range(4):
                    P.op(DVE, lambda e, b=b, tb=tb, hd=hd, w_=w_: e.tensor_tensor(out=w_[:, tb * 128:(tb + 1) * 128], in0=ps[:, b, tb * 128:(tb + 1) * 128],
                                                                              in1=evbc[:, 2048 + hd * 128:2048 + (hd + 1) * 128], op=ALU.add),
                         reads=[("ps", b), "evbc"], writes=[wkey])
                P.op(DVE, lambda e, hd=hd, w_=w_: e.tensor_tensor(out=AB[:, 8 + hd, :], in0=w_[:], in1=AB[:, 8 + hd, :], op=ALU.mult),
                     reads=[wkey, ("AB", 8 + hd)], writes=[("AB", 8 + hd)])
            for i in range(8):
                slot, skey = w_next("w_dg", j * 8 + i)
                b = ps_next()
                pairs = [(slot[:, k * 128:(k + 1) * 128], abuf[:, i, k:k + T], [skey, ("abuf", i)]) for k in range(31)]
                mm_group(ps[:, b, :], ("ps", b), pairs)
                cb = C_CB + j * 8 + i
                P.op(ACT, lambda e, b=b, i=i, cb=cb: e.activation(out=ybuf[:, i, :], in_=ps[:, b, :], func=AF.Identity, bias=cfm[:, cb:cb + 1], scale=1.0),
                     reads=[("ps", b), "cfm"], writes=[("ybuf", i)])
            bm, bq = ps_next(), ps_next()
            for i in range(8):
                w_ = wk[i % 2]
                P.op(ACT, lambda e, i=i, w_=w_: e.activation(out=w_[:], in_=ybuf[:, i, :], func=AF.Square), reads=[("ybuf", i)], writes=[("wk", i % 2)])
                P.op(PE, lambda e, i=i, bm=bm: e.matmul(ps[:, bm, :], onesf[:], ybuf[:, i, :], start=(i == 0), stop=(i == 7)),
                     reads=[("ybuf", i), "onesf"], writes=[("ps", bm)], signal=True)
                P.op(PE, lambda e, i=i, bq=bq, w_=w_: e.matmul(ps[:, bq, :], onesf[:], w_[:], start=(i == 0), stop=(i == 7)),
                     reads=[("wk", i % 2), "onesf"], writes=[("ps", bq)], signal=True)
            mean, var = wk[2], wk[3]
            P.op(ACT, lambda e: e.activation(out=mean[:], in_=ps[:, bm, :], func=AF.Copy), reads=[("ps", bm)], writes=[("wk", 2)])
            P.op(ACT, lambda e: e.activation(out=var[:], in_=ps[:, bm, :], func=AF.Square), reads=[("ps", bm)], writes=[("wk", 3)])
            P.op(DVE, lambda e: e.tensor_tensor(out=var[:], in0=ps[:, bq, :], in1=var[:], op=ALU.subtract), reads=[("ps", bq), ("wk", 3)], writes=[("wk", 3)])
            P.op(ACT, lambda e: e.activation(out=var[:], in_=var[:], func=AF.Sqrt, bias=epsc[:], scale=1.0), reads=[("wk", 3), "epsc"], writes=[("wk", 3)])
            P.op(DVE, lambda e: e.reciprocal(out=var[:], in_=var[:]), reads=[("wk", 3)], writes=[("wk", 3)])
            for i in range(8):
                w_ = wk[i % 2]
                ga, ba = C_LAG + j * 8 + i, C_LAB + j * 8 + i
                P.op(DVE, lambda e, i=i, w_=w_: e.tensor_tensor(out=w_[:], in0=ybuf[:, i, :], in1=mean[:], op=ALU.subtract),
                     reads=[("ybuf", i), ("wk", 2)], writes=[("wk", i % 2)])
                P.op(DVE, lambda e, w_=w_: e.tensor_tensor(out=w_[:], in0=w_[:], in1=var[:], op=ALU.mult),
                     reads=[("wk", i % 2), ("wk", 3)], writes=[("wk", i % 2)])
                P.op(ACT, lambda e, i=i, w_=w_, ga=ga, ba=ba: e.activation(out=AB[:, i, :], in_=w_[:], func=AF.Silu, scale=cfm[:, ga:ga + 1], bias=cfm[:, ba:ba + 1]),
                     reads=[("wk", i % 2), "cfm"], writes=[("AB", i)])
            if l == layers[0] and wpos[0] < len(sched):
                dump("AB", AB, [("AB", c) for c in range(16)], [128, 16, T], BF16)
                dump("hmix", h[:], [("h", c) for c in range(KC)], [128, KC, T], BF16)
                dump("ybuf", ybuf, [("ybuf", c) for c in range(8)], [128, 8, T], F32)
                dump("vnb", vnb, [("vnb", c) for c in range(4)], [128, 4, 1024], BF16)
            for g in range(8):
                slot, skey = w_next("w_out", j * 8 + g)
                for o in range(2):
                    b = proj_fm(slot, skey, o, lambda kc: AB[:, kc, :], lambda kc: ("AB", kc))
                    resid_add(2 * g + o, b)

        def odd_mixer(l, t):
            j = l // 2
            P.barrier()
            rmsnorm(C_MIXG + l * 16, h, "h")
            units = [(hd, c, kt) for hd in range(8) for c in range(2) for kt in range(t + 1)]
            loads = [u for u in units if u[2] < t]
            lpos = [0]
            lslot = {}

            def issue_load():
                i = lpos[0]
                if i >= len(loads):
                    return
                hd, c, kt = loads[i]
                s_ = i % NKV
                tk1 = P.dma(SP, "kl%d" % s_, lambda e, s_=s_, hm=2 * hd + c, kt=kt: e.dma_start(out=kslot[s_][:], in_=kc_d[j, hm, kt]),
                            reads=[("kc", j, kt)], writes=[("kslot", s_)])
                tk2 = P.dma(SP, "vl%d" % s_, lambda e, s_=s_, hd=hd, kt=kt: e.dma_start(out=vslot[s_][:], in_=vc_d[j, hd, kt].rearrange("p (a b) -> p a b", a=4)),
                            reads=[("vc", j, hd, kt)], writes=[("vslot", s_)])
                P.region_dma += [tk1, tk2]
                lslot[loads[i]] = s_
                lpos[0] += 1

            def ensure(i):
                while lpos[0] <= i and lpos[0] < len(loads):
                    issue_load()

            ensure(2)
            for hd in range(8):
                slot, skey = w_next("w_qkv", j * 24 + hd)
                for c in range(2):
                    b = proj_fm(slot, skey, c, lambda kc: h[:, kc, :], lambda kc: ("h", kc))
                    eng = evac_engine()
                    if eng == ACT:
                        P.op(ACT, lambda e, b=b, hm=2 * hd + c: e.activation(out=qT[:, hm, :], in_=ps[:, b, :], func=AF.Copy), reads=[("ps", b)], writes=[("qT", 2 * hd + c)])
                    else:
                        P.op(DVE, lambda e, b=b, hm=2 * hd + c: e.tensor_copy(out=qT[:, hm, :], in_=ps[:, b, :]), reads=[("ps", b)], writes=[("qT", 2 * hd + c)])
            for hd in range(8):
                slot, skey = w_next("w_qkv", j * 24 + 8 + hd)
                for c in range(2):
                    b = proj_fm(slot, skey, c, lambda kc: h[:, kc, :], lambda kc: ("h", kc))
                    eng = evac_engine()
                    if eng == ACT:
                        P.op(ACT, lambda e, b=b, hm=2 * hd + c: e.activation(out=kT[:, hm, :], in_=ps[:, b, :], func=AF.Copy), reads=[("ps", b)], writes=[("kT", 2 * hd + c)])
                    else:
                        P.op(DVE, lambda e, b=b, hm=2 * hd + c: e.tensor_copy(out=kT[:, hm, :], in_=ps[:, b, :]), reads=[("ps", b)], writes=[("kT", 2 * hd + c)])
            if t < ntiles - 1:
                tk_ = P.dma(SP, "kvw", lambda e: e.dma_start(out=kc_d[j, :, t].rearrange("hm d s -> d hm s"), in_=kT[:]),
                            reads=[("kT", hm) for hm in range(16)], writes=[("kc", j, t)])
                P.region_dma.append(tk_)
            if True:
                P.op(DVE, lambda e: e.memset(vcur[:, :, :, 256:257], 1.0), writes=[("vcur", hd) for hd in range(8)])
            for hd in range(8):
                slot, skey = w_next("w_qkv", j * 24 + 16 + hd)
                for tb in range(4):
                    b = ps_next()
                    pairs = [(h[:, kc, tb * 128:(tb + 1) * 128], slot[:, kc * 256:(kc + 1) * 256], [skey, ("h", kc)]) for kc in range(KC)]
                    mm_group(ps[:, b, 0:256], ("ps", b), pairs)
                    eng = ACT if hd % 2 == 0 else DVE
                    if eng == ACT:
                        P.op(ACT, lambda e, b=b, tb=tb, hd=hd: e.activation(out=vcur[:, tb, hd, 0:256], in_=ps[:, b, 0:256], func=AF.Copy), reads=[("ps", b)], writes=[("vcur", hd)])
                    else:
                        P.op(DVE, lambda e, b=b, tb=tb, hd=hd: e.tensor_copy(out=vcur[:, tb, hd, 0:256], in_=ps[:, b, 0:256]), reads=[("ps", b)], writes=[("vcur", hd)])
                if t < ntiles - 1:
                    tk_ = P.dma(SP, "kvw", lambda e, hd=hd: e.dma_start(out=vc_d[j, hd, t].rearrange("p (a b) -> p a b", a=4), in_=vcur[:, :, hd, :]),
                                reads=[("vcur", hd)], writes=[("vc", j, hd, t)])
                    P.region_dma.append(tk_)
            if t == 0 and l == layers[0]:
                dump("qT", qT, [("qT", c) for c in range(16)], [128, 16, T], BF16)
                dump("kT", kT, [("kT", c) for c in range(16)], [128, 16, T], BF16)
                dump("vcur", vcur, [("vcur", c) for c in range(8)], [128, 4, 8, 260], BF16)
            ps_mod[0] = 4
            pt_i = [0]
            def_b, def_t = [], []
            for hd in range(8):
                for c in range(2):
                    hm = 2 * hd + c
                    blocks = [(kt, kb) for kt in range(t + 1) for kb in range(4)]
                    pendq = []
                    nblk = 0

                    def score(kt, kb):
                        diag = (kt == t)
                        q0 = kb * 128 if diag else 0
                        if diag:
                            kap, kkey = kT[:, hm, kb * 128:(kb + 1) * 128], ("kT", hm)
                        else:
                            s_ = lslot[(hd, c, kt)]
                            kap, kkey = kslot[s_][:, kb * 128:(kb + 1) * 128], ("kslot", s_)
                        b = ps_next()
                        P.op(PE, lambda e, b=b, kap=kap, q0=q0, hm=hm: e.matmul(ps[:, b, q0:T], kap, qT[:, hm, q0:T], start=True, stop=True),
                             reads=[kkey, ("qT", hm)], writes=[("ps", b)], signal=True)
                        pi = pt_i[0] % 4
                        pt_i[0] += 1
                        pt = PT[pi]
                        P.op(ACT, lambda e, b=b, pt=pt, q0=q0: e.activation(out=pt[:, q0:T], in_=ps[:, b, q0:T], func=AF.Exp, scale=SCALE),
                             reads=[("ps", b)], writes=[("PT", pi)])
                        if diag:
                            P.op(DVE, lambda e, pt=pt, kb=kb: e.tensor_tensor(out=pt[:, kb * 128:(kb + 1) * 128], in0=pt[:, kb * 128:(kb + 1) * 128], in1=maskb[:], op=ALU.mult),
                                 reads=[("PT", pi), "maskb"], writes=[("PT", pi)])
                        return (kt, kb, pi)

                    def evac_a(qb):
                        so = 20 + qb
                        P.op(DVE, lambda e, qb=qb, so=so: e.reciprocal(out=small[:, so:so + 1], in_=ps[:, 4 + qb, 256:257]), reads=[("ps", 4 + qb)], writes=[("smz", qb)])
                        if c == 0:
                            P.op(DVE, lambda e, qb=qb, so=so: e.tensor_scalar(out=O1n[:, qb, :], in0=ps[:, 4 + qb, 0:256], scalar1=small[:, so:so + 1], scalar2=None, op0=ALU.mult),
                                 reads=[("ps", 4 + qb), ("smz", qb)], writes=[("O1n", qb)])
                        else:
                            P.op(DVE, lambda e, so=so: e.tensor_tensor(out=small[:, so:so + 1], in0=small[:, so:so + 1], in1=neglam[:, j:j + 1], op=ALU.mult),
                                 reads=[("smz", qb), "neglam"], writes=[("smz", qb)])
                            P.op(DVE, lambda e, qb=qb, so=so: e.scalar_tensor_tensor(out=Od[qb][:], in0=ps[:, 4 + qb, 0:256], scalar=small[:, so:so + 1], in1=O1n[:, qb, :],
                                                                                  op0=ALU.mult, op1=ALU.add),
                                 reads=[("ps", 4 + qb), ("smz", qb), ("O1n", qb)], writes=[("Od", qb)])

                    def av(kt, kb, pi):
                        diag = (kt == t)
                        if diag:
                            vap_of, vkey = (lambda kb_: vcur[:, kb_, hd, 0:257]), ("vcur", hd)
                        else:
                            s_ = lslot[(hd, c, kt)]
                            vap_of, vkey = (lambda kb_, s_=s_: vslot[s_][:, kb_, 0:257]), ("vslot", s_)
                        qb0 = kb if diag else 0
                        for qb in range(qb0, 4):
                            first = (kt == 0 and kb == 0)
                            last = (diag and kb == qb)
                            P.op(PE, lambda e, qb=qb, pi=pi, vap=vap_of(kb), first=first, last=last: e.matmul(ps[:, 4 + qb, 0:257], PT[pi][:, qb * 128:(qb + 1) * 128], vap, start=first, stop=last),
                                 reads=[("PT", pi), vkey], writes=[("ps", 4 + qb)], signal=(last or qb == 3))
                        if diag:
                            evac_a(kb)

                    for (kt, kb) in blocks:
                        if kb == 0 and kt < t:
                            ensure((hd * 2 + c) * t + kt + 2)
                        pendq.append(score(kt, kb))
                        if len(pendq) > 2:
                            av(*pendq.pop(0))
                        nblk += 1
                        if nblk == min(3, len(blocks)):
                            dl = def_b if c == 0 else def_t
                            for f_ in dl:
                                f_()
                            dl.clear()
                        if kb == 3 and kt < t:
                            pass
                    while pendq:
                        av(*pendq.pop(0))
                    if c == 1:
                        def phase_b(hd=hd):
                            for qb in range(4):
                                s2 = 28 + qb
                                P.op(DVE, lambda e, s2=s2: e.memset(small[:, s2:s2 + 1], 0.0), writes=[("sms", qb)])
                            for qb in range(4):
                                s2 = 28 + qb
                                P.op(ACT, lambda e, qb=qb, s2=s2: e.activation(out=On[qb][:], in_=Od[qb][:], func=AF.Square, accum_out=small[:, s2:s2 + 1]),
                                     reads=[("Od", qb)], writes=[("On", qb), ("sms", qb)])
                            for qb in range(4):
                                s2 = 28 + qb
                                P.op(ACT, lambda e, s2=s2: e.activation(out=small[:, s2:s2 + 1], in_=small[:, s2:s2 + 1], func=AF.Sqrt, bias=epsc[:], scale=1.0 / 256.0),
                                     reads=[("sms", qb), "epsc"], writes=[("sms", qb)])
                            for qb in range(4):
                                s2 = 28 + qb
                                P.op(DVE, lambda e, s2=s2: e.reciprocal(out=small[:, s2:s2 + 1], in_=small[:, s2:s2 + 1]), reads=[("sms", qb)], writes=[("sms", qb)])
                            for qb in range(4):
                                s2 = 28 + qb
                                P.op(DVE, lambda e, qb=qb, s2=s2: e.scalar_tensor_tensor(out=On[qb][:], in0=Od[qb][:], scalar=small[:, s2:s2 + 1], in1=sgb[:, j, :], op0=ALU.mult, op1=ALU.mult),
                                     reads=[("Od", qb), ("sms", qb), "sgb"], writes=[("On", qb)])

                        def tr_out(hd=hd):
                            for qb in range(4):
                                for hf in range(2):
                                    b = ps_next()
                                    P.op(PE, lambda e, b=b, qb=qb, hf=hf: e.transpose(ps[:, b, 0:128], On[qb][:, hf * 128:(hf + 1) * 128], ident[:]),
                                         reads=[("On", qb), "ident"], writes=[("ps", b)], signal=True)
                                    ci = 2 * hd + hf
                                    P.op(DVE, lambda e, b=b, ci=ci, qb=qb: e.tensor_copy(out=oT[:, ci, qb * 128:(qb + 1) * 128], in_=ps[:, b, 0:128]),
                                         reads=[("ps", b)], writes=[("oT", ci)])
                        def_b.append(phase_b)
                        def_t.append(tr_out)
            if t == 0 and l == layers[0]:
                dump("neglam", neglam[:], ["neglam"], [128, 2], F32)
            for f_ in def_b + def_t:
                f_()
            def_b.clear()
            def_t.clear()
            ps_mod[0] = 8
            for g in range(8):
                slot, skey = w_next("w_o", j * 8 + g)
                for o in range(2):
                    b = proj_fm(slot, skey, o, lambda kc: oT[:, kc, :], lambda kc: ("oT", kc))
                    resid_add(2 * g + o, b)

        def load_x(t):
            P.dma(SP, "xin", lambda e, t=t: e.dma_start(out=xin[:], in_=x_d[t * T:(t + 1) * T, :].rearrange("(a p) d -> p a d", p=128)),
                  writes=["xin"], region=True)

        for t in range(ntiles):
            if t == 0:
                P.barrier()
                load_x(0)
            for c in range(KC):
                b = ps_next()
                for tb in range(4):
                    P.op(PE, lambda e, b=b, tb=tb, c=c: e.transpose(ps[:, b, tb * 128:(tb + 1) * 128], xin[:, tb, c * 128:(c + 1) * 128], ident[:]),
                         reads=["xin", "ident"], writes=[("ps", b)], signal=(tb == 3))
                if evac_engine() == ACT:
                    P.op(ACT, lambda e, b=b, c=c: e.activation(out=x[:, c, :], in_=ps[:, b, :], func=AF.Copy), reads=[("ps", b)], writes=[("x", c)])
                else:
                    P.op(DVE, lambda e, b=b, c=c: e.tensor_copy(out=x[:, c, :], in_=ps[:, b, :]), reads=[("ps", b)], writes=[("x", c)])
            for l in layers:
                if not DEBUG["mixer"]:
                    pass
                elif l % 2 == 0:
                    even_mixer(l)
                else:
                    odd_mixer(l, t)
                if t == 0 and l == layers[0]:
                    dump("xmix", x[:], [("x", c) for c in range(KC)], [128, KC, T], F32)
                if DEBUG["ffn"]:
                    ffn(l, t)
            P.barrier()
            if t + 1 < ntiles:
                load_x(t + 1)
            if final:
                rmsnorm(C_FING, x, "x")
            src, skey_of = x, (lambda c: ("x", c))
            for tb in range(4):
                for cg in range(4):
                    b = ps_next()
                    for cc in range(4):
                        c = cg * 4 + cc
                        P.op(PE, lambda e, b=b, tb=tb, c=c, cc=cc: e.transpose(ps[:, b, cc * 128:(cc + 1) * 128], src[:, c, tb * 128:(tb + 1) * 128], ident[:]),
                             reads=[skey_of(c), "ident"], writes=[("ps", b)], signal=(cc == 3))
                    if tb % 2 == 0:
                        P.op(ACT, lambda e, b=b, tb=tb, cg=cg: e.activation(out=yout[:, tb, cg * 512:(cg + 1) * 512], in_=ps[:, b, :], func=AF.Copy),
                             reads=[("ps", b)], writes=[("yout", tb)])
                    else:
                        P.op(DVE, lambda e, b=b, tb=tb, cg=cg: e.tensor_copy(out=yout[:, tb, cg * 512:(cg + 1) * 512], in_=ps[:, b, :]),
                             reads=[("ps", b)], writes=[("yout", tb)])
            tk_ = P.dma(SP, "yst", lambda e, t=t: e.dma_start(out=y_d[t * T:(t + 1) * T, :].rearrange("(a p) d -> p a d", p=128), in_=yout[:]),
                        reads=[("yout", tb) for tb in range(4)], writes=[("ydram", t)])
            P.region_dma.append(tk_)
        P.barrier()
        P.emit(blk)
    return nc


def _groupify(W, col_lists):
    K = W.shape[0]
    kc = K // 128
    out = np.empty((len(col_lists), 128, kc * len(col_lists[0])), np.float32)
    for g, cols in enumerate(col_lists):
        Wg = W[:, cols].reshape(kc, 128, len(cols)).transpose(1, 0, 2)
        out[g] = Wg.reshape(128, -1)
    return out


def _fm(v):
    v = np.asarray(v, np.float32)
    lead = v.shape[:-1]
    n = v.shape[-1] // 128
    return np.moveaxis(v.reshape(lead + (n, 128)), -1, 0)


def prepare_consts(inp):
    r = np.arange
    w_in, w_out, w_qkv, w_o, w_up, w_down = [], [], [], [], [], []
    for j in range(2):
        W = inp["ev_w_in"][j]
        cl = [np.concatenate([r(i * 128, (i + 1) * 128), 1024 + r(i * 128, (i + 1) * 128)]) for i in range(8)]
        cl += [3072 + r(vg * 256, (vg + 1) * 256) for vg in range(4)]
        cl += [2048 + r(ug * 256, (ug + 1) * 256) for ug in range(4)]
        w_in.append(_groupify(W, cl))
        w_out.append(_groupify(inp["ev_w_out"][j], [r(g * 256, (g + 1) * 256) for g in range(8)]))
        w_qkv.append(_groupify(inp["od_w_qkv"][j], [r(g * 256, (g + 1) * 256) for g in range(24)]))
        w_o.append(_groupify(inp["od_w_o"][j], [r(g * 256, (g + 1) * 256) for g in range(8)]))
    for l in range(4):
        W = inp["ffn_w_up"][l]
        w_up.append(_groupify(W, [np.concatenate([r(g * 128, (g + 1) * 128), DFF + r(g * 128, (g + 1) * 128)]) for g in range(NFF)]))
        Wd = inp["ffn_w_down"][l]
        gd = np.empty((32, 128, 2816), np.float32)
        for m in range(16):
            for half in range(2):
                blk = Wd[half * 2816:(half + 1) * 2816, m * 128:(m + 1) * 128].reshape(22, 128, 128).transpose(1, 0, 2)
                gd[m * 2 + half] = blk.reshape(128, -1)
        w_down.append(gd)
    cfm = np.zeros((128, NFM), np.float32)
    cfm[:, C_MIXG:C_MIXG + 64] = _fm(inp["norm_mix_g"]).reshape(128, 64)
    cfm[:, C_FFNG:C_FFNG + 64] = _fm(inp["norm_ffn_g"]).reshape(128, 64)
    cfm[:, C_FING:C_FING + 16] = _fm(inp["final_norm_g"]).reshape(128, 16)
    cw = _fm(inp["ev_conv_w"])
    cfm[:, C_CW:C_CW + 496] = cw.transpose(0, 1, 3, 2).reshape(128, 496)
    cfm[:, C_CB:C_CB + 16] = _fm(inp["ev_conv_b"]).reshape(128, 16)
    cfm[:, C_LAG:C_LAG + 16] = _fm(inp["ev_ln_a_g"]).reshape(128, 16)
    cfm[:, C_LAB:C_LAB + 16] = _fm(inp["ev_ln_a_b"]).reshape(128, 16)
    fw = _fm(inp["ffn_conv_w"])
    cfm[:, C_FW:C_FW + 1056] = fw.transpose(0, 1, 3, 2).reshape(128, 1056)
    cfm[:, C_FB:C_FB + 352] = _fm(inp["ffn_conv_b"]).reshape(128, 352)
    cbc = np.zeros((1, NBC), np.float32)
    for j in range(2):
        o = B_EV + j * 3072
        cbc[0, o:o + 1024] = inp["ev_ln_v_g"][j]
        cbc[0, o + 1024:o + 2048] = inp["ev_ln_v_b"][j]
        cbc[0, o + 2048:o + 3072] = np.asarray(inp["ev_b_s"][j]).reshape(-1)
    cbc[0, B_SUB:B_SUB + 512] = np.asarray(inp["od_subln_g"]).reshape(-1)
    for w, name in enumerate(["od_lambda_q1", "od_lambda_k1", "od_lambda_q2", "od_lambda_k2"]):
        cbc[0, B_LAM + w * 256:B_LAM + (w + 1) * 256] = np.asarray(inp[name]).reshape(-1)
    wst = np.ascontiguousarray(np.asarray(inp["ev_w_s"], np.float32).transpose(0, 1, 3, 2)).reshape(16, 128, 128)
    idm = np.stack([np.eye(128, dtype=np.float32), np.triu(np.ones((128, 128), np.float32))])
    return {
        "w_in": np.concatenate(w_in), "w_out": np.concatenate(w_out), "w_qkv": np.concatenate(w_qkv),
        "w_o": np.concatenate(w_o), "w_up": np.concatenate(w_up), "w_down": np.concatenate(w_down),
        "cfm": cfm, "cbc": cbc, "wst": wst, "idm": idm,
    }


LAUNCH_PLAN = [([0, 1, 2, 3], True)]


def run_plan(inp, plan, cores=8, ntiles=NT, trace=False):
    consts = prepare_consts(inp)
    xs = [np.ascontiguousarray(np.asarray(inp["x"][b], np.float32)) for b in range(cores)]
    res = None
    for layers, final in plan:
        nc = build_program(layers, final, ntiles)
        in_maps = [dict(consts, x=xs[b]) for b in range(cores)]
        res = run_bass_kernel_spmd(nc, in_maps, core_ids=list(range(cores)), trace=trace)
        xs = [np.asarray(res.results[b]["y"], np.float32) for b in range(cores)]
    return xs, res


LAST_RES = None


def kernel(**inputs):
    xs, _ = run_plan(inputs, LAUNCH_PLAN, cores=8)
    return np.stack(xs, axis=0).astype(np.float32)
```

```python
import contextlib
import math
import numpy as np
import ml_dtypes
import concourse.bass as bass
import concourse.mybir as mybir
from concourse.bass_utils import run_bass_kernel_spmd

F32 = mybir.dt.float32
BF16 = mybir.dt.bfloat16
AF = mybir.ActivationFunctionType
ALU = mybir.AluOpType
AX = mybir.AxisListType

PE, ACT, DVE, POOL, SP = "pe", "act", "dve", "pool", "sp"
ENGS = (PE, ACT, DVE, POOL, SP)

S = 4096
D = 2048
T = 512
NT = S // T
KC = 16
DFF = 5632
NFF = DFF // 128
EPS = 1e-6
SCALE = 1.0 / math.sqrt(128.0)
NSLOT = 5
NKV = 4

C_MIXG, C_FFNG, C_FING, C_CW, C_CB, C_LAG, C_LAB, C_FW, C_FB, NFM = 0, 64, 128, 144, 640, 656, 672, 688, 1744, 2096
B_EV, B_SUB, B_LAM, NBC = 0, 6144, 6656, 7680


class Prog:
    def __init__(self, nc, stack):
        self.nc = nc
        self.stack = stack
        self.ops = {e: [] for e in ENGS}
        self.esem = {e: stack.enter_context(nc.semaphore("prog_" + e)) for e in (PE, ACT, DVE, POOL)}
        self.cnt = {e: 0 for e in (PE, ACT, DVE, POOL)}
        self.dsem = {}
        self.dcnt = {}
        self.waited = {e: {} for e in ENGS}
        self.last_w = {}
        self.readers = {}
        self.region_dma = []
        self.sim = {e: [] for e in ENGS}

    def dma_sem(self, name):
        if name not in self.dsem:
            self.dsem[name] = self.stack.enter_context(self.nc.semaphore("dma_" + name))
            self.dcnt[name] = 0
        return self.dsem[name]

    def wait(self, eng, tok):
        if tok is None:
            return
        kind, name, val = tok
        if kind == "e" and name == PE and eng == PE:
            return
        key = (kind, name)
        if self.waited[eng].get(key, 0) >= val:
            return
        self.waited[eng][key] = val
        sem = self.esem[name] if kind == "e" else self.dsem[name]
        self.ops[eng].append(lambda e, sem=sem, val=val: e.wait_ge(sem, val))
        self.sim[eng].append(("w", key, val))

    def _deps(self, eng, reads, writes, extra):
        for k in reads:
            self.wait(eng, self.last_w.get(k))
        for k in writes:
            self.wait(eng, self.last_w.get(k))
            for t in self.readers.get(k, ()):
                self.wait(eng, t)
        for t in extra:
            self.wait(eng, t)

    def _commit(self, tok, reads, writes):
        for k in reads:
            self.readers.setdefault(k, []).append(tok)
        for k in writes:
            self.last_w[k] = tok
            self.readers[k] = []

    def op(self, eng, fn, reads=(), writes=(), signal=True, extra=()):
        self._deps(eng, reads, writes, extra)
        if signal:
            self.cnt[eng] += 1
            tok = ("e", eng, self.cnt[eng])
            sem = self.esem[eng]
            self.ops[eng].append(lambda e, fn=fn, sem=sem: fn(e).then_inc(sem, 1))
            self.sim[eng].append(("i", ("e", eng), 1))
        else:
            tok = ("e", eng, self.cnt[eng] + 1)
            self.ops[eng].append(lambda e, fn=fn: fn(e))
        self._commit(tok, reads, writes)
        return tok

    def dma(self, q, semname, fn, reads=(), writes=(), extra=(), region=False):
        self._deps(q, reads, writes, extra)
        sem = self.dma_sem(semname)
        self.dcnt[semname] += 16
        tok = ("d", semname, self.dcnt[semname])
        self.ops[q].append(lambda e, fn=fn, sem=sem: fn(e).then_inc(sem, 16))
        self.sim[q].append(("i", ("d", semname), 16))
        self._commit(tok, reads, writes)
        if region:
            self.region_dma.append(tok)
        return tok

    def last_tok(self, eng):
        return ("e", eng, self.cnt[eng]) if self.cnt[eng] else None

    def barrier(self):
        toks = [self.last_tok(e) for e in (PE, ACT, DVE, POOL)] + list(self.region_dma)
        self.region_dma = []
        for eng in (PE, ACT, DVE, SP):
            for t in toks:
                self.wait(eng, t)

    def check_deadlock(self):
        pos = {e: 0 for e in ENGS}
        val = {}
        progress = True
        while progress:
            progress = False
            for e in ENGS:
                lst = self.sim[e]
                while pos[e] < len(lst):
                    k, key, v = lst[pos[e]]
                    if k == "w":
                        if val.get(key, 0) < v:
                            break
                    else:
                        val[key] = val.get(key, 0) + v
                    pos[e] += 1
                    progress = True
        stuck = {e: (pos[e], len(self.sim[e]), self.sim[e][pos[e]]) for e in ENGS if pos[e] < len(self.sim[e])}
        if stuck:
            raise RuntimeError("DEADLOCK in recorded program: %r" % (stuck,))

    def emit(self, block):
        self.check_deadlock()
        prog = self

        @block.tensor
        def _(e):
            for f in prog.ops[PE]:
                f(e)

        @block.scalar
        def _(e):
            for f in prog.ops[ACT]:
                f(e)

        @block.vector
        def _(e):
            for f in prog.ops[DVE]:
                f(e)

        @block.gpsimd
        def _(e):
            for f in prog.ops[POOL]:
                f(e)

        @block.sync
        def _(e):
            for f in prog.ops[SP]:
                f(e)


def lambda_init(i):
    return 0.8 - 0.6 * math.exp(-0.3 * i)


DEBUG = {"mixer": True, "ffn": True, "dump": False}


def weight_schedule(layers):
    seq = []
    for l in layers:
        j = l // 2
        if not DEBUG["mixer"]:
            pass
        elif l % 2 == 0:
            seq += [("w_in", j * 16 + g, 4096) for g in range(16)]
            seq += [("w_dg", j * 8 + g, 4096) for g in range(8)]
            seq += [("w_out", j * 8 + g, 4096) for g in range(8)]
        else:
            seq += [("w_qkv", j * 24 + g, 4096) for g in range(24)]
            seq += [("w_o", j * 8 + g, 4096) for g in range(8)]
        if DEBUG["ffn"]:
            seq += [("w_up", l * 44 + g, 4096) for g in range(44)]
            seq += [("w_down", l * 32 + g, 2816) for g in range(32)]
    return seq


def build_program(layers, final, ntiles=NT):
    nc = bass.Bass("TRN2", target_bir_lowering=False)
    dt = nc.dram_tensor
    x_d = dt("x", [S, D], F32, kind="ExternalInput").ap()
    y_d = dt("y", [S, D], F32, kind="ExternalOutput").ap()
    wshape = {"w_in": (32, 4096), "w_out": (16, 4096), "w_qkv": (48, 4096), "w_o": (16, 4096),
              "w_up": (176, 4096), "w_down": (128, 2816)}
    w_d = {k: dt(k, [g, 128, e], F32, kind="ExternalInput").ap() for k, (g, e) in wshape.items()}
    wb_d = {k: dt("b_" + k, [g, 128, e], BF16, kind="Internal").ap() for k, (g, e) in wshape.items()}
    wb_d["w_dg"] = dt("b_w_dg", [16, 128, 4096], BF16, kind="Internal").ap()
    cfm_d = dt("cfm", [128, NFM], F32, kind="ExternalInput").ap()
    cbc_d = dt("cbc", [1, NBC], F32, kind="ExternalInput").ap()
    wst_d = dt("wst", [16, 128, 128], F32, kind="ExternalInput").ap()
    idm_d = dt("idm", [2, 128, 128], F32, kind="ExternalInput").ap()
    kc_d = dt("kcache", [2, 16, NT, 128, T], BF16, kind="Internal").ap()
    vc_d = dt("vcache", [2, 8, NT, 128, 4 * 260], BF16, kind="Internal").ap()

    with contextlib.ExitStack() as st:
        P = Prog(nc, st)

        def sb(name, shape, dtype):
            return st.enter_context(nc.sbuf_tensor("sb_" + name, shape, dtype))

        x = sb("x", [128, KC, T], F32)
        h = sb("h", [128, KC, T], BF16)
        wslot = [sb("wslot%d" % i, [128, 4096], BF16) for i in range(NSLOT)]
        cfm = sb("cfm_sb", [128, NFM], F32)
        wsm = sb("wsm", [128, 16, 128], BF16)
        ident = sb("ident", [128, 128], F32)
        maskf = sb("maskf", [128, 128], F32)
        maskb = sb("maskb", [128, 128], BF16)
        identb = sb("identb", [128, 128], BF16)
        onesb = sb("onesb", [128, 128], BF16)
        onesf = sb("onesf", [128, 128], F32)
        epsc = sb("epsc", [128, 1], F32)
        sgb = sb("sgb", [128, 2, 256], F32)
        lamw = sb("lamw", [128, 8], F32)
        neglam = sb("neglam", [128, 2], F32)
        halo_f = sb("halo_f", [128, 4 * 88, 2], F32)
        halo_a = sb("halo_a", [128, 2, 8, 30], BF16)
        rstd = sb("rstd", [128, T], F32)
        sqb = [sb("sqb%d" % i, [128, T], BF16) for i in range(2)]
        small = sb("small", [128, 64], F32)
        R = sb("R", [128, 94 * 1024], mybir.dt.uint8)

        def rview(off_kib, shape, dtype):
            n = int(np.prod(shape[1:]))
            esz = 2 if dtype == BF16 else 4
            off = int(off_kib * 1024)
            v = R[:, off:off + n * esz].bitcast(dtype)
            if len(shape) == 3:
                v = v.rearrange("p (a b) -> p a b", a=shape[1])
            elif len(shape) == 4:
                v = v.rearrange("p (a b c) -> p a b c", a=shape[1], b=shape[2])
            return v

        wsf = rview(0, [128, 16, 128], F32)
        lamt = rview(8, [128, 8, 128], F32)
        G = rview(0, [128, NFF, T], BF16)
        zb = [[rview(44 + 4.25 * (2 * b + s_), [128, 520], F32) for s_ in range(2)] for b in range(2)]
        t1 = [[rview(61 + 2 * (2 * b + s_), [128, T], F32) for s_ in range(2)] for b in range(2)]
        qT = rview(0, [128, 16, T], BF16)
        kT = rview(16, [128, 16, T], BF16)
        vcur = rview(32, [128, 4, 8, 260], BF16)
        kslot = [rview(48.25 + i, [128, T], BF16) for i in range(NKV)]
        vslot = [rview(52.25 + 2.25 * i, [128, 4, 260], BF16) for i in range(NKV)]
        PT = [rview(61.25 + i, [128, T], BF16) for i in range(4)]
        O1n = rview(65.25, [128, 4, 256], F32)
        Od = [rview(69.25 + i, [128, 256], F32) for i in range(4)]
        On = [rview(73.25 + i, [128, 256], F32) for i in range(4)]
        oT = rview(77.25, [128, 16, T], BF16)
        abuf = rview(0, [128, 8, 544], BF16)
        dgst = [rview(16 + 8 * i, [128, 32, 128], BF16) for i in range(2)]
        evbc = rview(8.5, [128, 3072], F32)
        ybuf = rview(20.5, [128, 8, T], F32)
        vtok = rview(36.5, [128, 4, 1024], F32)
        vnb = rview(52.5, [128, 4, 1024], BF16)
        AB = rview(60.5, [128, 16, T], BF16)
        wk = [rview(76.5 + 2 * i, [128, T], F32) for i in range(5)]
        xin = rview(0, [128, 4, D], F32)
        xn = rview(32, [128, KC, T], F32)
        yout = rview(32, [128, 4, D], F32)

        ps = st.enter_context(nc.psum_tensor("ps", [128, 8, 512], F32))
        blk = st.enter_context(nc.Block())

        def dump(name, ap, keys, shape, dtype):
            if not DEBUG["dump"]:
                return
            d = nc.dram_tensor("dbg_" + name, shape, dtype, kind="ExternalOutput").ap()
            P.dma(SP, "dbg", lambda e: e.dma_start(out=d, in_=ap), reads=keys, region=True)

        ps_rr = [0]

        ps_mod = [8]

        def ps_next():
            i = ps_rr[0] % ps_mod[0]
            ps_rr[0] += 1
            return i

        cp_rr = [0]

        def evac_engine():
            cp_rr[0] += 1
            return ACT if cp_rr[0] % 2 else DVE

        P.dma(SP, "c0", lambda e: e.dma_start(out=cfm[:], in_=cfm_d), writes=["cfm"], region=True)
        P.dma(SP, "c1", lambda e: e.dma_start(out=ident[:], in_=idm_d[0]), writes=["ident"], region=True)
        P.dma(SP, "c2", lambda e: e.dma_start(out=maskf[:], in_=idm_d[1]), writes=["maskf"], region=True)
        P.dma(SP, "c3", lambda e: e.dma_start(out=wsf[:], in_=wst_d.rearrange("g s t -> s g t")), writes=["wsf"], region=True)
        P.dma(SP, "c4", lambda e: e.dma_start(out=sgb[:].rearrange("p a b -> p (a b)"),
                                              in_=cbc_d[:, B_SUB:B_SUB + 512].partition_broadcast(128)), writes=["sgb"], region=True)
        P.dma(SP, "c5", lambda e: e.dma_start(out=lamt[:].rearrange("p a b -> p (a b)"),
                                              in_=cbc_d[:, B_LAM:B_LAM + 1024].partition_broadcast(128)), writes=["lamt"], region=True)
        P.op(DVE, lambda e: e.memset(onesb[:], 1.0 / 2048.0), writes=["onesb"])
        P.op(DVE, lambda e: e.memset(onesf[:], 1.0 / 1024.0), writes=["onesf"])
        P.op(DVE, lambda e: e.memset(epsc[:], EPS), writes=["epsc"])
        P.op(DVE, lambda e: e.memset(halo_f[:], 0.0), writes=["halo_f"])
        P.op(DVE, lambda e: e.memset(halo_a[:], 0.0), writes=["halo_a"])
        P.op(DVE, lambda e: e.tensor_copy(out=maskb[:], in_=maskf[:]), reads=["maskf"], writes=["maskb"])
        P.op(DVE, lambda e: e.tensor_copy(out=identb[:], in_=ident[:]), reads=["ident"], writes=["identb"])
        if DEBUG["mixer"]:
            for g in sorted({l // 2 * 8 + i for l in layers if l % 2 == 0 for i in range(8)}):
                stg = dgst[g % 2]
                P.op(DVE, lambda e, stg=stg: e.memset(stg[:, 31, :], 0.0), writes=[("dgst", g % 2)])
                for k in range(31):
                    wc = C_CW + g * 31 + k
                    P.op(DVE, lambda e, stg=stg, k=k, wc=wc: e.tensor_scalar(out=stg[:, k, :], in0=identb[:], scalar1=cfm[:, wc:wc + 1], scalar2=None, op0=ALU.mult),
                         reads=["identb", "cfm"], writes=[("dgst", g % 2)])
                P.dma(SP, "dgw", lambda e, stg=stg, g=g: e.dma_start(out=wb_d["w_dg"][g], in_=stg[:].rearrange("p a b -> p (a b)")),
                      reads=[("dgst", g % 2)], writes=[("wbg", g)], region=True)
        for g in range(16):
            P.op(DVE, lambda e, g=g: e.tensor_tensor(out=wsm[:, g, :], in0=wsf[:, g, :], in1=maskf[:], op=ALU.mult),
                 reads=["wsf", "maskf"], writes=["wsm"])
        for j in range(2):
            li = lambda_init(2 * j + 1)
            for a, (qi, ki) in enumerate(((0 + j, 2 + j), (4 + j, 6 + j))):
                P.op(DVE, lambda e, qi=qi, ki=ki: e.tensor_tensor(out=lamt[:, qi, :], in0=lamt[:, qi, :], in1=lamt[:, ki, :], op=ALU.mult),
                     reads=["lamt"], writes=["lamt"])
                P.op(DVE, lambda e, qi=qi, c=2 * j + a: e.reduce_sum(out=lamw[:, c:c + 1], in_=lamt[:, qi, :], axis=AX.X),
                     reads=["lamt"], writes=["lamw"])
                P.op(ACT, lambda e, c=2 * j + a: e.activation(out=lamw[:, c:c + 1], in_=lamw[:, c:c + 1], func=AF.Exp),
                     reads=["lamw"], writes=["lamw"])
            P.op(DVE, lambda e, j=j: e.tensor_tensor(out=neglam[:, j:j + 1], in0=lamw[:, 2 * j + 1:2 * j + 2], in1=lamw[:, 2 * j:2 * j + 1], op=ALU.subtract),
                 reads=["lamw"], writes=["neglam"])
            P.op(DVE, lambda e, j=j, li=li: e.tensor_scalar(out=neglam[:, j:j + 1], in0=neglam[:, j:j + 1], scalar1=-li, scalar2=None, op0=ALU.add),
                 reads=["neglam"], writes=["neglam"])
            P.op(DVE, lambda e, j=j, li=li: e.tensor_scalar(out=sgb[:, j, :], in0=sgb[:, j, :], scalar1=1.0 - li, scalar2=None, op0=ALU.mult),
                 reads=["sgb"], writes=["sgb"])

        sched = weight_schedule(layers)
        full = sched * ntiles
        wpos = [0]
        wissued = [0]

        def w_issue():
            i = wissued[0]
            if i >= len(full):
                return
            kind, g, e_ = full[i]
            s_ = i % NSLOT
            if i < len(sched) and kind != "w_dg":
                P.dma(POOL, "wc%d" % s_, lambda e, kind=kind, g=g, e_=e_, s_=s_: e.dma_start(out=wslot[s_][:, 0:e_], in_=w_d[kind][g]),
                      writes=[("ws", s_)])
                if ntiles > 1:
                    P.dma(SP, "wk%d" % s_, lambda e, kind=kind, g=g, e_=e_, s_=s_: e.dma_start(out=wb_d[kind][g], in_=wslot[s_][:, 0:e_]),
                          reads=[("ws", s_)], writes=[("wbg", kind, g)])
            else:
                P.dma(SP, "w%d" % s_, lambda e, kind=kind, g=g, e_=e_, s_=s_: e.dma_start(out=wslot[s_][:, 0:e_], in_=wb_d[kind][g]),
                      reads=[("wbg", g) if kind == "w_dg" else ("wbg", kind, g)], writes=[("ws", s_)])
            wissued[0] += 1

        def w_next(kind, g):
            i = wpos[0]
            assert full[i][0] == kind and full[i][1] == g, (full[i], kind, g)
            while wissued[0] < min(i + NSLOT, len(full)):
                w_issue()
            wpos[0] += 1
            s_ = i % NSLOT
            return wslot[s_], ("ws", s_)

        def mm_group(out_ap, out_key, pairs):
            n = len(pairs)
            tok = None
            for i, (l_, r_, rk) in enumerate(pairs):
                tok = P.op(PE, lambda e, l_=l_, r_=r_, i=i: e.matmul(out_ap, l_, r_, start=(i == 0), stop=(i == n - 1)),
                           reads=rk, writes=[out_key], signal=(i == n - 1))
            return tok

        def rmsnorm(gcol, out, out_key):
            b = ps_next()
            for c in range(KC):
                sq = sqb[c % 2]
                P.op(ACT, lambda e, c=c, sq=sq: e.activation(out=sq[:], in_=x[:, c, :], func=AF.Square),
                     reads=[("x", c)], writes=[("sqb", c % 2)])
                P.op(PE, lambda e, c=c, sq=sq, b=b: e.matmul(ps[:, b, :], onesb[:], sq[:], start=(c == 0), stop=(c == KC - 1)),
                     reads=[("sqb", c % 2), "onesb"], writes=[("ps", b)], signal=True)
            P.op(ACT, lambda e, b=b: e.activation(out=rstd[:], in_=ps[:, b, :], func=AF.Sqrt, bias=epsc[:], scale=1.0),
                 reads=[("ps", b), "epsc"], writes=["rstd"])
            P.op(DVE, lambda e: e.reciprocal(out=rstd[:], in_=rstd[:]), reads=["rstd"], writes=["rstd"])
            for c in range(KC):
                P.op(DVE, lambda e, c=c: e.scalar_tensor_tensor(out=out[:, c, :], in0=x[:, c, :], scalar=cfm[:, gcol + c:gcol + c + 1],
                                                                in1=rstd[:], op0=ALU.mult, op1=ALU.mult),
                     reads=[("x", c), "rstd", "cfm"], writes=[(out_key, c)])

        def proj_fm(slot, skey, o, rhs_of, rhs_key_of, nk=KC, width=256):
            b = ps_next()
            pairs = [(slot[:, kc * width + o * 128: kc * width + (o + 1) * 128], rhs_of(kc), [skey, rhs_key_of(kc)]) for kc in range(nk)]
            mm_group(ps[:, b, :], ("ps", b), pairs)
            return b

        def proj_fm2(slot, skey, rhs_of, rhs_key_of, nk=KC, width=256):
            b0, b1 = ps_next(), ps_next()
            for kc in range(nk):
                for o, b in ((0, b0), (1, b1)):
                    l_ = slot[:, kc * width + o * 128: kc * width + (o + 1) * 128]
                    P.op(PE, lambda e, b=b, l_=l_, r_=rhs_of(kc), kc=kc: e.matmul(ps[:, b, :], l_, r_, start=(kc == 0), stop=(kc == nk - 1)),
                         reads=[skey, rhs_key_of(kc)], writes=[("ps", b)], signal=(kc == nk - 1))
            return b0, b1

        def resid_add(m, b):
            P.op(DVE, lambda e, m=m, b=b: e.tensor_tensor(out=x[:, m, :], in0=x[:, m, :], in1=ps[:, b, :], op=ALU.add),
                 reads=[("ps", b), ("x", m)], writes=[("x", m)])

        def ffn(l, t=1):
            P.barrier()
            rmsnorm(C_FFNG + l * 16, h, "h")
            if t == 0 and l == layers[0]:
                dump("hffn", h[:], [("h", c) for c in range(KC)], [128, KC, T], BF16)
            for j in range(NFF):
                slot, skey = w_next("w_up", l * 44 + j)
                bb = j % 2
                banks = list(proj_fm2(slot, skey, lambda kc: h[:, kc, :], lambda kc: ("h", kc)))
                for s_ in range(2):
                    b = banks[s_]
                    jj = j + 44 * s_
                    z = zb[bb][s_]
                    tt = t1[bb][s_]
                    zk, tk = ("zb", bb, s_), ("t1", bb, s_)
                    hcol = l * 88 + jj
                    wc = C_FW + hcol * 3
                    bc = C_FB + hcol
                    P.op(DVE, lambda e, z=z, hcol=hcol: e.tensor_copy(out=z[:, 0:2], in_=halo_f[:, hcol, :]),
                         reads=[("halo_f", hcol)], writes=[zk])
                    P.op(ACT, lambda e, z=z, b=b: e.activation(out=z[:, 2:2 + T], in_=ps[:, b, :], func=AF.Copy),
                         reads=[("ps", b)], writes=[zk])
                    P.op(ACT, lambda e, tt=tt, b=b, wc=wc, bc=bc: e.activation(out=tt[:], in_=ps[:, b, :], func=AF.Identity,
                                                                              scale=cfm[:, wc + 2:wc + 3], bias=cfm[:, bc:bc + 1]),
                         reads=[("ps", b), "cfm"], writes=[tk])
                    P.op(DVE, lambda e, z=z, hcol=hcol: e.tensor_copy(out=halo_f[:, hcol, :], in_=z[:, T:T + 2]),
                         reads=[zk], writes=[("halo_f", hcol)])
                    P.op(DVE, lambda e, z=z, tt=tt, wc=wc: e.scalar_tensor_tensor(out=tt[:], in0=z[:, 1:1 + T], scalar=cfm[:, wc + 1:wc + 2],
                                                                               in1=tt[:], op0=ALU.mult, op1=ALU.add),
                         reads=[zk, tk, "cfm"], writes=[tk])
                    P.op(DVE, lambda e, z=z, tt=tt, wc=wc: e.scalar_tensor_tensor(out=tt[:], in0=z[:, 0:T], scalar=cfm[:, wc:wc + 1],
                                                                               in1=tt[:], op0=ALU.mult, op1=ALU.add),
                         reads=[zk, tk, "cfm"], writes=[tk])
                tg, tv = t1[bb][0], t1[bb][1]
                P.op(ACT, lambda e, tg=tg: e.activation(out=tg[:], in_=tg[:], func=AF.Silu),
                     reads=[("t1", bb, 0)], writes=[("t1", bb, 0)])
                P.op(DVE, lambda e, tg=tg, tv=tv, j=j: e.tensor_tensor(out=G[:, j, :], in0=tg[:], in1=tv[:], op=ALU.mult),
                     reads=[("t1", bb, 0), ("t1", bb, 1)], writes=[("G", j)])
            if t == 0 and l == layers[0]:
                dump("G", G, [("G", c) for c in range(NFF)], [128, NFF, T], BF16)
            for m in range(KC):
                b = ps_next()
                for half in range(2):
                    slot, skey = w_next("w_down", l * 32 + m * 2 + half)
                    for kk in range(22):
                        kc = half * 22 + kk
                        P.op(PE, lambda e, b=b, slot=slot, kk=kk, kc=kc: e.matmul(ps[:, b, :], slot[:, kk * 128:(kk + 1) * 128], G[:, kc, :],
                                                                                 start=(kc == 0), stop=(kc == 43)),
                             reads=[skey, ("G", kc)], writes=[("ps", b)], signal=(kc == 43))
                resid_add(m, b)

        def even_mixer(l):
            j = l // 2
            P.barrier()
            tok = P.dma(SP, "evbc", lambda e: e.dma_start(out=evbc[:], in_=cbc_d[:, B_EV + j * 3072:B_EV + (j + 1) * 3072].partition_broadcast(128)),
                        writes=["evbc"], region=True)
            rmsnorm(C_MIXG + l * 16, h, "h")
            P.op(DVE, lambda e: e.tensor_copy(out=abuf[:, :, 0:30], in_=halo_a[:, j, :, :]), reads=["halo_a"], writes=[("abuf", i) for i in range(8)])
            for i in range(8):
                slot, skey = w_next("w_in", j * 16 + i)
                bv, bg = proj_fm2(slot, skey, lambda kc: h[:, kc, :], lambda kc: ("h", kc))
                w_ = wk[i % 2]
                P.op(ACT, lambda e, w_=w_, bg=bg: e.activation(out=w_[:], in_=ps[:, bg, :], func=AF.Sigmoid),
                     reads=[("ps", bg)], writes=[("wk", i % 2)])
                P.op(DVE, lambda e, w_=w_, bv=bv, i=i: e.tensor_tensor(out=abuf[:, i, 30:30 + T], in0=ps[:, bv, :], in1=w_[:], op=ALU.mult),
                     reads=[("ps", bv), ("wk", i % 2)], writes=[("abuf", i)])
            P.op(DVE, lambda e: e.tensor_copy(out=halo_a[:, j, :, :], in_=abuf[:, :, T:T + 30]), reads=[("abuf", i) for i in range(8)], writes=["halo_a"])
            for vg in range(4):
                slot, skey = w_next("w_in", j * 16 + 8 + vg)
                for tb in range(4):
                    b = ps_next()
                    pairs = [(h[:, kc, tb * 128:(tb + 1) * 128], slot[:, kc * 256:(kc + 1) * 256], [skey, ("h", kc)]) for kc in range(KC)]
                    mm_group(ps[:, b, 0:256], ("ps", b), pairs)
                    P.op(ACT, lambda e, b=b, tb=tb, vg=vg: e.activation(out=vtok[:, tb, vg * 256:(vg + 1) * 256], in_=ps[:, b, 0:256], func=AF.Gelu_apprx_tanh),
                         reads=[("ps", b)], writes=[("vtok", tb, vg)])
            for tb in range(4):
                vk = [("vtok", tb, vg) for vg in range(4)]
                so = tb * 16
                for hh in range(2):
                    P.op(DVE, lambda e, tb=tb, hh=hh, so=so: e.bn_stats(out=small[:, so + hh * 6:so + hh * 6 + 6], in_=vtok[:, tb, hh * 512:(hh + 1) * 512]),
                         reads=vk, writes=[("small", tb)])
                P.op(DVE, lambda e, so=so: e.bn_aggr(out=small[:, so + 12:so + 14], in_=small[:, so:so + 12]), reads=[("small", tb)], writes=[("small", tb)])
                P.op(ACT, lambda e, so=so: e.activation(out=small[:, so + 14:so + 15], in_=small[:, so + 13:so + 14], func=AF.Sqrt, bias=epsc[:], scale=1.0),
                     reads=[("small", tb), "epsc"], writes=[("small", tb)])
                P.op(DVE, lambda e, so=so: e.reciprocal(out=small[:, so + 14:so + 15], in_=small[:, so + 14:so + 15]), reads=[("small", tb)], writes=[("small", tb)])
                P.op(DVE, lambda e, tb=tb, so=so: e.tensor_scalar(out=vtok[:, tb, :], in0=vtok[:, tb, :], scalar1=small[:, so + 12:so + 13],
                                                                 scalar2=small[:, so + 14:so + 15], op0=ALU.subtract, op1=ALU.mult),
                     reads=vk + [("small", tb)], writes=vk)
                P.op(DVE, lambda e, tb=tb: e.tensor_tensor(out=vtok[:, tb, :], in0=vtok[:, tb, :], in1=evbc[:, 0:1024], op=ALU.mult),
                     reads=vk + ["evbc"], writes=vk)
                P.op(DVE, lambda e, tb=tb: e.tensor_tensor(out=vnb[:, tb, :], in0=vtok[:, tb, :], in1=evbc[:, 1024:2048], op=ALU.add),
                     reads=vk + ["evbc"], writes=[("vnb", tb)])
            for ug in range(4):
                slot, skey = w_next("w_in", j * 16 + 12 + ug)
                for o in range(2):
                    b = proj_fm(slot, skey, o, lambda kc: h[:, kc, :], lambda kc: ("h", kc))
                    ci = 8 + 2 * ug + o
                    P.op(ACT, lambda e, b=b, ci=ci: e.activation(out=AB[:, ci, :], in_=ps[:, b, :], func=AF.Gelu_apprx_tanh),
                         reads=[("ps", b)], writes=[("AB", ci)])
            for hd in range(8):
                b = ps_next()
                for tb in range(4):
                    P.op(PE, lambda e, b=b, tb=tb, hd=hd: e.matmul(ps[:, b, tb * 128:(tb + 1) * 128], vnb[:, tb, hd * 128:(hd + 1) * 128],
                                                                   wsm[:, j * 8 + hd, :], start=True, stop=True),
                         reads=[("vnb", tb), "wsm"], writes=[("ps", b)], signal=(tb == 3))
                w_ = wk[2 + hd % 2]
                wkey = ("wk", 2 + hd % 2)
                for tb in range(4):
                    P.op(DVE, lambda e, b=b, tb=tb, hd=hd, w_=w_: e.tensor_tensor(out=w_[:, tb * 128:(tb + 1) * 128], in0=ps[:, b, tb * 128:(tb + 1) * 128],
                                                                              in1=evbc[:, 2048 + hd * 128:2048 + (hd + 1) * 128], op=ALU.add),
                         reads=[("ps", b), "evbc"], writes=[wkey])
                P.op(DVE, lambda e, hd=hd, w_=w_: e.tensor_tensor(out=AB[:, 8 + hd, :], in0=w_[:], in1=AB[:, 8 + hd, :], op=ALU.mult),
                     reads=[wkey, ("AB", 8 + hd)], writes=[("AB", 8 + hd)])
            for i in range(8):
                slot, skey = w_next("w_dg", j * 8 + i)
                b = ps_next()
                pairs = [(slot[:, k * 128:(k + 1) * 128], abuf[:, i, k:k + T], [skey, ("abuf", i)]) for k in range(31)]
                mm_group(ps[:, b, :], ("ps", b), pairs)
                cb = C_CB + j * 8 + i
                P.op(ACT, lambda e, b=b, i=i, cb=cb: e.activation(out=ybuf[:, i, :], in_=ps[:, b, :], func=AF.Identity, bias=cfm[:, cb:cb + 1], scale=1.0),
                     reads=[("ps", b), "cfm"], writes=[("ybuf", i)])
            bm, bq = ps_next(), ps_next()
            for i in range(8):
                w_ = wk[i % 2]
                P.op(ACT, lambda e, i=i, w_=w_: e.activation(out=w_[:], in_=ybuf[:, i, :], func=AF.Square), reads=[("ybuf", i)], writes=[("wk", i % 2)])
                P.op(PE, lambda e, i=i, bm=bm: e.matmul(ps[:, bm, :], onesf[:], ybuf[:, i, :], start=(i == 0), stop=(i == 7)),
                     reads=[("ybuf", i), "onesf"], writes=[("ps", bm)], signal=True)
                P.op(PE, lambda e, i=i, bq=bq, w_=w_: e.matmul(ps[:, bq, :], onesf[:], w_[:], start=(i == 0), stop=(i == 7)),
                     reads=[("wk", i % 2), "onesf"], writes=[("ps", bq)], signal=True)
            mean, var = wk[2], wk[3]
            P.op(ACT, lambda e: e.activation(out=mean[:], in_=ps[:, bm, :], func=AF.Copy), reads=[("ps", bm)], writes=[("wk", 2)])
            P.op(ACT, lambda e: e.activation(out=var[:], in_=ps[:, bm, :], func=AF.Square), reads=[("ps", bm)], writes=[("wk", 3)])
            P.op(DVE, lambda e: e.tensor_tensor(out=var[:], in0=ps[:, bq, :], in1=var[:], op=ALU.subtract), reads=[("ps", bq), ("wk", 3)], writes=[("wk", 3)])
            P.op(ACT, lambda e: e.activation(out=var[:], in_=var[:], func=AF.Sqrt, bias=epsc[:], scale=1.0), reads=[("wk", 3), "epsc"], writes=[("wk", 3)])
            P.op(DVE, lambda e: e.reciprocal(out=var[:], in_=var[:]), reads=[("wk", 3)], writes=[("wk", 3)])
            for i in range(8):
                w_ = wk[i % 2]
                ga, ba = C_LAG + j * 8 + i, C_LAB + j * 8 + i
                P.op(DVE, lambda e, i=i, w_=w_: e.tensor_tensor(out=w_[:], in0=ybuf[:, i, :], in1=mean[:], op=ALU.subtract),
                     reads=[("ybuf", i), ("wk", 2)], writes=[("wk", i % 2)])
                P.op(DVE, lambda e, w_=w_: e.tensor_tensor(out=w_[:], in0=w_[:], in1=var[:], op=ALU.mult),
                     reads=[("wk", i % 2), ("wk", 3)], writes=[("wk", i % 2)])
                P.op(ACT, lambda e, i=i, w_=w_, ga=ga, ba=ba: e.activation(out=AB[:, i, :], in_=w_[:], func=AF.Silu, scale=cfm[:, ga:ga + 1], bias=cfm[:, ba:ba + 1]),
                     reads=[("wk", i % 2), "cfm"], writes=[("AB", i)])
            if l == layers[0] and wpos[0] < len(sched):
                dump("AB", AB, [("AB", c) for c in range(16)], [128, 16, T], BF16)
                dump("hmix", h[:], [("h", c) for c in range(KC)], [128, KC, T], BF16)
                dump("ybuf", ybuf, [("ybuf", c) for c in range(8)], [128, 8, T], F32)
                dump("vnb", vnb, [("vnb", c) for c in range(4)], [128, 4, 1024], BF16)
            for g in range(8):
                slot, skey = w_next("w_out", j * 8 + g)
                for o in range(2):
                    b = proj_fm(slot, skey, o, lambda kc: AB[:, kc, :], lambda kc: ("AB", kc))
                    resid_add(2 * g + o, b)

        def odd_mixer(l, t):
            j = l // 2
            P.barrier()
            rmsnorm(C_MIXG + l * 16, h, "h")
            units = [(hd, c, kt) for hd in range(8) for c in range(2) for kt in range(t + 1)]
            loads = [u for u in units if u[2] < t]
            lpos = [0]
            lslot = {}

            def issue_load():
                i = lpos[0]
                if i >= len(loads):
                    return
                hd, c, kt = loads[i]
                s_ = i % NKV
                tk1 = P.dma(SP, "kl%d" % s_, lambda e, s_=s_, hm=2 * hd + c, kt=kt: e.dma_start(out=kslot[s_][:], in_=kc_d[j, hm, kt]),
                            reads=[("kc", j, kt)], writes=[("kslot", s_)])
                tk2 = P.dma(SP, "vl%d" % s_, lambda e, s_=s_, hd=hd, kt=kt: e.dma_start(out=vslot[s_][:], in_=vc_d[j, hd, kt].rearrange("p (a b) -> p a b", a=4)),
                            reads=[("vc", j, hd, kt)], writes=[("vslot", s_)])
                P.region_dma += [tk1, tk2]
                lslot[loads[i]] = s_
                lpos[0] += 1

            def ensure(i):
                while lpos[0] <= i and lpos[0] < len(loads):
                    issue_load()

            ensure(2)
            for hd in range(8):
                slot, skey = w_next("w_qkv", j * 24 + hd)
                bpair = proj_fm2(slot, skey, lambda kc: h[:, kc, :], lambda kc: ("h", kc))
                for c in range(2):
                    b = bpair[c]
                    eng = evac_engine()
                    if eng == ACT:
                        P.op(ACT, lambda e, b=b, hm=2 * hd + c: e.activation(out=qT[:, hm, :], in_=ps[:, b, :], func=AF.Copy), reads=[("ps", b)], writes=[("qT", 2 * hd + c)])
                    else:
                        P.op(DVE, lambda e, b=b, hm=2 * hd + c: e.tensor_copy(out=qT[:, hm, :], in_=ps[:, b, :]), reads=[("ps", b)], writes=[("qT", 2 * hd + c)])
            for hd in range(8):
                slot, skey = w_next("w_qkv", j * 24 + 8 + hd)
                bpair = proj_fm2(slot, skey, lambda kc: h[:, kc, :], lambda kc: ("h", kc))
                for c in range(2):
                    b = bpair[c]
                    eng = evac_engine()
                    if eng == ACT:
                        P.op(ACT, lambda e, b=b, hm=2 * hd + c: e.activation(out=kT[:, hm, :], in_=ps[:, b, :], func=AF.Copy), reads=[("ps", b)], writes=[("kT", 2 * hd + c)])
                    else:
                        P.op(DVE, lambda e, b=b, hm=2 * hd + c: e.tensor_copy(out=kT[:, hm, :], in_=ps[:, b, :]), reads=[("ps", b)], writes=[("kT", 2 * hd + c)])
            if t < ntiles - 1:
                tk_ = P.dma(SP, "kvw", lambda e: e.dma_start(out=kc_d[j, :, t].rearrange("hm d s -> d hm s"), in_=kT[:]),
                            reads=[("kT", hm) for hm in range(16)], writes=[("kc", j, t)])
                P.region_dma.append(tk_)
            if True:
                P.op(DVE, lambda e: e.memset(vcur[:, :, :, 256:257], 1.0), writes=[("vcur", hd) for hd in range(8)])
            for hd in range(8):
                slot, skey = w_next("w_qkv", j * 24 + 16 + hd)
                for tb in range(4):
                    b = ps_next()
                    pairs = [(h[:, kc, tb * 128:(tb + 1) * 128], slot[:, kc * 256:(kc + 1) * 256], [skey, ("h", kc)]) for kc in range(KC)]
                    mm_group(ps[:, b, 0:256], ("ps", b), pairs)
                    eng = ACT if hd % 2 == 0 else DVE
                    if eng == ACT:
                        P.op(ACT, lambda e, b=b, tb=tb, hd=hd: e.activation(out=vcur[:, tb, hd, 0:256], in_=ps[:, b, 0:256], func=AF.Copy), reads=[("ps", b)], writes=[("vcur", hd)])
                    else:
                        P.op(DVE, lambda e, b=b, tb=tb, hd=hd: e.tensor_copy(out=vcur[:, tb, hd, 0:256], in_=ps[:, b, 0:256]), reads=[("ps", b)], writes=[("vcur", hd)])
                if t < ntiles - 1:
                    tk_ = P.dma(SP, "kvw", lambda e, hd=hd: e.dma_start(out=vc_d[j, hd, t].rearrange("p (a b) -> p a b", a=4), in_=vcur[:, :, hd, :]),
                                reads=[("vcur", hd)], writes=[("vc", j, hd, t)])
                    P.region_dma.append(tk_)
            if t == 0 and l == layers[0]:
                dump("qT", qT, [("qT", c) for c in range(16)], [128, 16, T], BF16)
                dump("kT", kT, [("kT", c) for c in range(16)], [128, 16, T], BF16)
                dump("vcur", vcur, [("vcur", c) for c in range(8)], [128, 4, 8, 260], BF16)
            ps_mod[0] = 4
            pt_i = [0]
            def_b, def_t = [], []
            for hd in range(8):
                for c in range(2):
                    hm = 2 * hd + c
                    blocks = [(kt, kb) for kt in range(t + 1) for kb in range(4)]
                    pendq = []
                    nblk = 0

                    def score(kt, kb):
                        diag = (kt == t)
                        q0 = kb * 128 if diag else 0
                        if diag:
                            kap, kkey = kT[:, hm, kb * 128:(kb + 1) * 128], ("kT", hm)
                        else:
                            s_ = lslot[(hd, c, kt)]
                            kap, kkey = kslot[s_][:, kb * 128:(kb + 1) * 128], ("kslot", s_)
                        b = ps_next()
                        P.op(PE, lambda e, b=b, kap=kap, q0=q0, hm=hm: e.matmul(ps[:, b, q0:T], kap, qT[:, hm, q0:T], start=True, stop=True),
                             reads=[kkey, ("qT", hm)], writes=[("ps", b)], signal=True)
                        pi = pt_i[0] % 4
                        pt_i[0] += 1
                        pt = PT[pi]
                        P.op(ACT, lambda e, b=b, pt=pt, q0=q0: e.activation(out=pt[:, q0:T], in_=ps[:, b, q0:T], func=AF.Exp, scale=SCALE),
                             reads=[("ps", b)], writes=[("PT", pi)])
                        if diag:
                            P.op(DVE, lambda e, pt=pt, kb=kb: e.tensor_tensor(out=pt[:, kb * 128:(kb + 1) * 128], in0=pt[:, kb * 128:(kb + 1) * 128], in1=maskb[:], op=ALU.mult),
                                 reads=[("PT", pi), "maskb"], writes=[("PT", pi)])
                        return (kt, kb, pi)

                    def evac_a(qb):
                        so = 20 + qb
                        P.op(DVE, lambda e, qb=qb, so=so: e.reciprocal(out=small[:, so:so + 1], in_=ps[:, 4 + qb, 256:257]), reads=[("ps", 4 + qb)], writes=[("smz", qb)])
                        if c == 0:
                            P.op(DVE, lambda e, qb=qb, so=so: e.tensor_scalar(out=O1n[:, qb, :], in0=ps[:, 4 + qb, 0:256], scalar1=small[:, so:so + 1], scalar2=None, op0=ALU.mult),
                                 reads=[("ps", 4 + qb), ("smz", qb)], writes=[("O1n", qb)])
                        else:
                            P.op(DVE, lambda e, so=so: e.tensor_tensor(out=small[:, so:so + 1], in0=small[:, so:so + 1], in1=neglam[:, j:j + 1], op=ALU.mult),
                                 reads=[("smz", qb), "neglam"], writes=[("smz", qb)])
                            P.op(DVE, lambda e, qb=qb, so=so: e.scalar_tensor_tensor(out=Od[qb][:], in0=ps[:, 4 + qb, 0:256], scalar=small[:, so:so + 1], in1=O1n[:, qb, :],
                                                                                  op0=ALU.mult, op1=ALU.add),
                                 reads=[("ps", 4 + qb), ("smz", qb), ("O1n", qb)], writes=[("Od", qb)])

                    def av(kt, kb, pi):
                        diag = (kt == t)
                        if diag:
                            vap_of, vkey = (lambda kb_: vcur[:, kb_, hd, 0:257]), ("vcur", hd)
                        else:
                            s_ = lslot[(hd, c, kt)]
                            vap_of, vkey = (lambda kb_, s_=s_: vslot[s_][:, kb_, 0:257]), ("vslot", s_)
                        qb0 = kb if diag else 0
                        for qb in range(qb0, 4):
                            first = (kt == 0 and kb == 0)
                            last = (diag and kb == qb)
                            P.op(PE, lambda e, qb=qb, pi=pi, vap=vap_of(kb), first=first, last=last: e.matmul(ps[:, 4 + qb, 0:257], PT[pi][:, qb * 128:(qb + 1) * 128], vap, start=first, stop=last),
                                 reads=[("PT", pi), vkey], writes=[("ps", 4 + qb)], signal=(last or qb == 3))
                        if diag:
                            evac_a(kb)

                    for (kt, kb) in blocks:
                        if kb == 0 and kt < t:
                            ensure((hd * 2 + c) * t + kt + 2)
                        pendq.append(score(kt, kb))
                        if len(pendq) > 2:
                            av(*pendq.pop(0))
                        nblk += 1
                        if nblk == min(3, len(blocks)):
                            dl = def_b if c == 0 else def_t
                            for f_ in dl:
                                f_()
                            dl.clear()
                        if kb == 3 and kt < t:
                            pass
                    while pendq:
                        av(*pendq.pop(0))
                    if c == 1:
                        def phase_b(hd=hd):
                            for qb in range(4):
                                s2 = 28 + qb
                                P.op(DVE, lambda e, s2=s2: e.memset(small[:, s2:s2 + 1], 0.0), writes=[("sms", qb)])
                            for qb in range(4):
                                s2 = 28 + qb
                                P.op(ACT, lambda e, qb=qb, s2=s2: e.activation(out=On[qb][:], in_=Od[qb][:], func=AF.Square, accum_out=small[:, s2:s2 + 1]),
                                     reads=[("Od", qb)], writes=[("On", qb), ("sms", qb)])
                            for qb in range(4):
                                s2 = 28 + qb
                                P.op(ACT, lambda e, s2=s2: e.activation(out=small[:, s2:s2 + 1], in_=small[:, s2:s2 + 1], func=AF.Sqrt, bias=epsc[:], scale=1.0 / 256.0),
                                     reads=[("sms", qb), "epsc"], writes=[("sms", qb)])
                            for qb in range(4):
                                s2 = 28 + qb
                                P.op(DVE, lambda e, s2=s2: e.reciprocal(out=small[:, s2:s2 + 1], in_=small[:, s2:s2 + 1]), reads=[("sms", qb)], writes=[("sms", qb)])
                            for qb in range(4):
                                s2 = 28 + qb
                                P.op(DVE, lambda e, qb=qb, s2=s2: e.scalar_tensor_tensor(out=On[qb][:], in0=Od[qb][:], scalar=small[:, s2:s2 + 1], in1=sgb[:, j, :], op0=ALU.mult, op1=ALU.mult),
                                     reads=[("Od", qb), ("sms", qb), "sgb"], writes=[("On", qb)])

                        def tr_out(hd=hd):
                            for qb in range(4):
                                for hf in range(2):
                                    b = ps_next()
                                    P.op(PE, lambda e, b=b, qb=qb, hf=hf: e.transpose(ps[:, b, 0:128], On[qb][:, hf * 128:(hf + 1) * 128], ident[:]),
                                         reads=[("On", qb), "ident"], writes=[("ps", b)], signal=True)
                                    ci = 2 * hd + hf
                                    P.op(DVE, lambda e, b=b, ci=ci, qb=qb: e.tensor_copy(out=oT[:, ci, qb * 128:(qb + 1) * 128], in_=ps[:, b, 0:128]),
                                         reads=[("ps", b)], writes=[("oT", ci)])
                        def_b.append(phase_b)
                        def_t.append(tr_out)
            if t == 0 and l == layers[0]:
                dump("neglam", neglam[:], ["neglam"], [128, 2], F32)
            for f_ in def_b + def_t:
                f_()
            def_b.clear()
            def_t.clear()
            ps_mod[0] = 8
            for g in range(8):
                slot, skey = w_next("w_o", j * 8 + g)
                for o in range(2):
                    b = proj_fm(slot, skey, o, lambda kc: oT[:, kc, :], lambda kc: ("oT", kc))
                    resid_add(2 * g + o, b)

        def load_x(t):
            P.dma(SP, "xin", lambda e, t=t: e.dma_start(out=xin[:], in_=x_d[t * T:(t + 1) * T, :].rearrange("(a p) d -> p a d", p=128)),
                  writes=["xin"], region=True)

        for t in range(ntiles):
            if t == 0:
                P.barrier()
                load_x(0)
            for c in range(KC):
                b = ps_next()
                for tb in range(4):
                    P.op(PE, lambda e, b=b, tb=tb, c=c: e.transpose(ps[:, b, tb * 128:(tb + 1) * 128], xin[:, tb, c * 128:(c + 1) * 128], ident[:]),
                         reads=["xin", "ident"], writes=[("ps", b)], signal=(tb == 3))
                if evac_engine() == ACT:
                    P.op(ACT, lambda e, b=b, c=c: e.activation(out=x[:, c, :], in_=ps[:, b, :], func=AF.Copy), reads=[("ps", b)], writes=[("x", c)])
                else:
                    P.op(DVE, lambda e, b=b, c=c: e.tensor_copy(out=x[:, c, :], in_=ps[:, b, :]), reads=[("ps", b)], writes=[("x", c)])
            for l in layers:
                if not DEBUG["mixer"]:
                    pass
                elif l % 2 == 0:
                    even_mixer(l)
                else:
                    odd_mixer(l, t)
                if t == 0 and l == layers[0]:
                    dump("xmix", x[:], [("x", c) for c in range(KC)], [128, KC, T], F32)
                if DEBUG["ffn"]:
                    ffn(l, t)
            P.barrier()
            if t + 1 < ntiles:
                load_x(t + 1)
            if final:
                rmsnorm(C_FING, x, "x")
            src, skey_of = x, (lambda c: ("x", c))
            for tb in range(4):
                for cg in range(4):
                    b = ps_next()
                    for cc in range(4):
                        c = cg * 4 + cc
                        P.op(PE, lambda e, b=b, tb=tb, c=c, cc=cc: e.transpose(ps[:, b, cc * 128:(cc + 1) * 128], src[:, c, tb * 128:(tb + 1) * 128], ident[:]),
                             reads=[skey_of(c), "ident"], writes=[("ps", b)], signal=(cc == 3))
                    if tb % 2 == 0:
                        P.op(ACT, lambda e, b=b, tb=tb, cg=cg: e.activation(out=yout[:, tb, cg * 512:(cg + 1) * 512], in_=ps[:, b, :], func=AF.Copy),
                             reads=[("ps", b)], writes=[("yout", tb)])
                    else:
                        P.op(DVE, lambda e, b=b, tb=tb, cg=cg: e.tensor_copy(out=yout[:, tb, cg * 512:(cg + 1) * 512], in_=ps[:, b, :]),
                             reads=[("ps", b)], writes=[("yout", tb)])
            tk_ = P.dma(SP, "yst", lambda e, t=t: e.dma_start(out=y_d[t * T:(t + 1) * T, :].rearrange("(a p) d -> p a d", p=128), in_=yout[:]),
                        reads=[("yout", tb) for tb in range(4)], writes=[("ydram", t)])
            P.region_dma.append(tk_)
        P.barrier()
        P.emit(blk)
    return nc


def _groupify(W, col_lists):
    K = W.shape[0]
    kc = K // 128
    out = np.empty((len(col_lists), 128, kc * len(col_lists[0])), np.float32)
    for g, cols in enumerate(col_lists):
        Wg = W[:, cols].reshape(kc, 128, len(cols)).transpose(1, 0, 2)
        out[g] = Wg.reshape(128, -1)
    return out


def _fm(v):
    v = np.asarray(v, np.float32)
    lead = v.shape[:-1]
    n = v.shape[-1] // 128
    return np.moveaxis(v.reshape(lead + (n, 128)), -1, 0)


def prepare_consts(inp):
    r = np.arange
    w_in, w_out, w_qkv, w_o, w_up, w_down = [], [], [], [], [], []
    for j in range(2):
        W = inp["ev_w_in"][j]
        cl = [np.concatenate([r(i * 128, (i + 1) * 128), 1024 + r(i * 128, (i + 1) * 128)]) for i in range(8)]
        cl += [3072 + r(vg * 256, (vg + 1) * 256) for vg in range(4)]
        cl += [2048 + r(ug * 256, (ug + 1) * 256) for ug in range(4)]
        w_in.append(_groupify(W, cl))
        w_out.append(_groupify(inp["ev_w_out"][j], [r(g * 256, (g + 1) * 256) for g in range(8)]))
        w_qkv.append(_groupify(inp["od_w_qkv"][j], [r(g * 256, (g + 1) * 256) for g in range(24)]))
        w_o.append(_groupify(inp["od_w_o"][j], [r(g * 256, (g + 1) * 256) for g in range(8)]))
    for l in range(4):
        W = inp["ffn_w_up"][l]
        w_up.append(_groupify(W, [np.concatenate([r(g * 128, (g + 1) * 128), DFF + r(g * 128, (g + 1) * 128)]) for g in range(NFF)]))
        Wd = inp["ffn_w_down"][l]
        gd = np.empty((32, 128, 2816), np.float32)
        for m in range(16):
            for half in range(2):
                blk = Wd[half * 2816:(half + 1) * 2816, m * 128:(m + 1) * 128].reshape(22, 128, 128).transpose(1, 0, 2)
                gd[m * 2 + half] = blk.reshape(128, -1)
        w_down.append(gd)
    cfm = np.zeros((128, NFM), np.float32)
    cfm[:, C_MIXG:C_MIXG + 64] = _fm(inp["norm_mix_g"]).reshape(128, 64)
    cfm[:, C_FFNG:C_FFNG + 64] = _fm(inp["norm_ffn_g"]).reshape(128, 64)
    cfm[:, C_FING:C_FING + 16] = _fm(inp["final_norm_g"]).reshape(128, 16)
    cw = _fm(inp["ev_conv_w"])
    cfm[:, C_CW:C_CW + 496] = cw.transpose(0, 1, 3, 2).reshape(128, 496)
    cfm[:, C_CB:C_CB + 16] = _fm(inp["ev_conv_b"]).reshape(128, 16)
    cfm[:, C_LAG:C_LAG + 16] = _fm(inp["ev_ln_a_g"]).reshape(128, 16)
    cfm[:, C_LAB:C_LAB + 16] = _fm(inp["ev_ln_a_b"]).reshape(128, 16)
    fw = _fm(inp["ffn_conv_w"])
    cfm[:, C_FW:C_FW + 1056] = fw.transpose(0, 1, 3, 2).reshape(128, 1056)
    cfm[:, C_FB:C_FB + 352] = _fm(inp["ffn_conv_b"]).reshape(128, 352)
    cbc = np.zeros((1, NBC), np.float32)
    for j in range(2):
        o = B_EV + j * 3072
        cbc[0, o:o + 1024] = inp["ev_ln_v_g"][j]
        cbc[0, o + 1024:o + 2048] = inp["ev_ln_v_b"][j]
        cbc[0, o + 2048:o + 3072] = np.asarray(inp["ev_b_s"][j]).reshape(-1)
    cbc[0, B_SUB:B_SUB + 512] = np.asarray(inp["od_subln_g"]).reshape(-1)
    for w, name in enumerate(["od_lambda_q1", "od_lambda_k1", "od_lambda_q2", "od_lambda_k2"]):
        cbc[0, B_LAM + w * 256:B_LAM + (w + 1) * 256] = np.asarray(inp[name]).reshape(-1)
    wst = np.ascontiguousarray(np.asarray(inp["ev_w_s"], np.float32).transpose(0, 1, 3, 2)).reshape(16, 128, 128)
    idm = np.stack([np.eye(128, dtype=np.float32), np.triu(np.ones((128, 128), np.float32))])
    return {
        "w_in": np.concatenate(w_in), "w_out": np.concatenate(w_out), "w_qkv": np.concatenate(w_qkv),
        "w_o": np.concatenate(w_o), "w_up": np.concatenate(w_up), "w_down": np.concatenate(w_down),
        "cfm": cfm, "cbc": cbc, "wst": wst, "idm": idm,
    }


LAUNCH_PLAN = [([0, 1, 2, 3], True)]


def run_plan(inp, plan, cores=8, ntiles=NT, trace=False):
    consts = prepare_consts(inp)
    xs = [np.ascontiguousarray(np.asarray(inp["x"][b], np.float32)) for b in range(cores)]
    res = None
    for layers, final in plan:
        nc = build_program(layers, final, ntiles)
        in_maps = [dict(consts, x=xs[b]) for b in range(cores)]
        res = run_bass_kernel_spmd(nc, in_maps, core_ids=list(range(cores)), trace=trace)
        xs = [np.asarray(res.results[b]["y"], np.float32) for b in range(cores)]
    return xs, res


LAST_RES = None


def kernel(**inputs):
    xs, _ = run_plan(inputs, LAUNCH_PLAN, cores=8)
    return np.stack(xs, axis=0).astype(np.float32)
```
